# Optimizing a Trainium2 kernel written in Bass

```python
import math
import jax, jax.numpy as jnp
from jax import lax
import numpy as np

D_MODEL = 1024
BATCH = 8
SEQ = 2048
DEPTH = 1
DEC_BATCH = 128
DEC_SEQ = 8
PAST_LEN = 16384
PAGE_SIZE = 128

MIX_DIM = D_MODEL
SSM_DIM = MIX_DIM // 2
CONV_DIM = MIX_DIM - SSM_DIM
SSM_GROUP_CH = 16
SSM_GROUPS = SSM_DIM // SSM_GROUP_CH
SSM_STATE = 64
CONV_WIDTH = 31
N_MEM = 256
MEM_HEADS = 4
MEM_HEAD_DIM = D_MODEL // MEM_HEADS
D_FF = ((8 * D_MODEL // 3 + 127) // 128) * 128
EPS = 1e-6
DT_MIN = 1e-3
DT_MAX = 1e-1

kernel_name = "hymba_s5_conformer_macaron_memxattn_step"


def rms_norm(x, g):
    xf = x.astype(jnp.float32)
    y = xf * lax.rsqrt(jnp.mean(xf * xf, axis=-1, keepdims=True) + EPS)
    return (y * g.astype(jnp.float32)).astype(x.dtype)


def layer_norm(x, g, b):
    xf = x.astype(jnp.float32)
    mu = jnp.mean(xf, axis=-1, keepdims=True)
    var = jnp.mean(jnp.square(xf - mu), axis=-1, keepdims=True)
    y = (xf - mu) * lax.rsqrt(var + EPS) * g.astype(jnp.float32) + b.astype(jnp.float32)
    return y.astype(x.dtype)


def swiglu_ffn(x, w_gate, w_up, w_down):
    return (jax.nn.silu(x @ w_gate) * (x @ w_up)) @ w_down


def s5_mixer(u, h0_re, h0_im, a_re, a_im, log_dt, b_re, b_im, c_re, c_im, d_skip, w_glu):
    f32 = jnp.float32
    bsz, t_len, _ = u.shape
    dt = jnp.exp(log_dt.astype(f32))[:, None]
    lr, li = a_re.astype(f32), a_im.astype(f32)
    mag = jnp.exp(lr * dt)
    ab_re, ab_im = mag * jnp.cos(li * dt), mag * jnp.sin(li * dt)
    den = lr * lr + li * li
    nr, ni = ab_re - 1.0, ab_im
    coef_re = (nr * lr + ni * li) / den
    coef_im = (ni * lr - nr * li) / den
    uf = u.astype(f32)
    ug = uf.reshape(bsz, t_len, SSM_GROUPS, SSM_GROUP_CH)
    bu_re = jnp.einsum('btgc,gnc->btgn', ug, b_re.astype(f32))
    bu_im = jnp.einsum('btgc,gnc->btgn', ug, b_im.astype(f32))
    x_re = coef_re * bu_re - coef_im * bu_im
    x_im = coef_re * bu_im + coef_im * bu_re
    a_re_t = jnp.broadcast_to(ab_re, x_re.shape)
    a_im_t = jnp.broadcast_to(ab_im, x_re.shape)

    def combine(e1, e2):
        a1r, a1i, b1r, b1i = e1
        a2r, a2i, b2r, b2i = e2
        return (a1r * a2r - a1i * a2i,
                a1r * a2i + a1i * a2r,
                a2r * b1r - a2i * b1i + b2r,
                a2r * b1i + a2i * b1r + b2i)

    p_re, p_im, s_re, s_im = lax.associative_scan(combine, (a_re_t, a_im_t, x_re, x_im), axis=1)
    h0r = h0_re.astype(f32)[:, None]
    h0i = h0_im.astype(f32)[:, None]
    h_re = s_re + p_re * h0r - p_im * h0i
    h_im = s_im + p_re * h0i + p_im * h0r
    y = (jnp.einsum('btgn,gcn->btgc', h_re, c_re.astype(f32))
         - jnp.einsum('btgn,gcn->btgc', h_im, c_im.astype(f32)))
    y = y.reshape(bsz, t_len, SSM_DIM) + d_skip.astype(f32) * uf
    g = jax.nn.gelu(y)
    out = g * jax.nn.sigmoid(g @ w_glu.astype(f32))
    return out.astype(u.dtype), h_re[:, -1], h_im[:, -1]


def conformer_conv_mixer(val, gate, buf, w_dw, b_dw, ln_g, ln_b):
    v = val * jax.nn.sigmoid(gate)
    xp = jnp.concatenate([buf.astype(v.dtype), v], axis=1)
    y = lax.conv_general_dilated(xp, w_dw[:, None, :].astype(v.dtype), window_strides=(1,),
                                 padding='VALID', dimension_numbers=('NWC', 'WIO', 'NWC'),
                                 feature_group_count=CONV_DIM) + b_dw
    y = jax.nn.silu(layer_norm(y, ln_g, ln_b))
    return y, xp[:, -(CONV_WIDTH - 1):]


def mem_kv(mem, g_mem, w_k, w_v):
    bsz = mem.shape[0]
    m = rms_norm(mem, g_mem)
    k = (m @ w_k).reshape(bsz, N_MEM, MEM_HEADS, MEM_HEAD_DIM)
    v = (m @ w_v).reshape(bsz, N_MEM, MEM_HEADS, MEM_HEAD_DIM)
    return k, v


def mem_attend(h, k, v, w_q, w_o):
    bsz, t_len, _ = h.shape
    q = (h @ w_q).reshape(bsz, t_len, MEM_HEADS, MEM_HEAD_DIM)
    s = jnp.einsum('bthd,bmhd->bhtm', q, k.astype(q.dtype)).astype(jnp.float32) * (MEM_HEAD_DIM ** -0.5)
    p = jax.nn.softmax(s, axis=-1).astype(h.dtype)
    o = jnp.einsum('bhtm,bmhd->bthd', p, v.astype(h.dtype)).reshape(bsz, t_len, MEM_HEADS * MEM_HEAD_DIM)
    return o @ w_o


def decoder_layer(x, mk, mv, h0_re, h0_im, conv_buf,
                  g_ffn1, w_ffn1_gate, w_ffn1_up, w_ffn1_down,
                  g_mix, w_in,
                  ssm_a_re, ssm_a_im, ssm_log_dt, ssm_b_re, ssm_b_im, ssm_c_re, ssm_c_im, ssm_d, w_ssm_glu,
                  conv_w, conv_b, conv_ln_g, conv_ln_b,
                  w_out, g_xattn, w_mem_q, w_mem_o,
                  g_ffn2, w_ffn2_gate, w_ffn2_up, w_ffn2_down):
    x = x + 0.5 * swiglu_ffn(rms_norm(x, g_ffn1), w_ffn1_gate, w_ffn1_up, w_ffn1_down)
    h = rms_norm(x, g_mix)
    z = h @ w_in
    u_ssm = z[..., :SSM_DIM]
    c_val = z[..., SSM_DIM:SSM_DIM + CONV_DIM]
    c_gate = z[..., SSM_DIM + CONV_DIM:]
    s_out, h_re, h_im = s5_mixer(u_ssm, h0_re, h0_im, ssm_a_re, ssm_a_im, ssm_log_dt,
                                 ssm_b_re, ssm_b_im, ssm_c_re, ssm_c_im, ssm_d, w_ssm_glu)
    c_out, new_buf = conformer_conv_mixer(c_val, c_gate, conv_buf, conv_w, conv_b, conv_ln_g, conv_ln_b)
    x = x + jnp.concatenate([s_out, c_out], axis=-1) @ w_out
    x = x + mem_attend(rms_norm(x, g_xattn), mk, mv, w_mem_q, w_mem_o)
    x = x + 0.5 * swiglu_ffn(rms_norm(x, g_ffn2), w_ffn2_gate, w_ffn2_up, w_ffn2_down)
    return x, h_re.astype(x.dtype), h_im.astype(x.dtype), new_buf


def setup_inputs(seed: int = 0) -> dict:
    key = jax.random.key(seed)
    ks = iter(jax.random.split(key, 64))
    f32 = jnp.float32

    def nrm(shape, scale):
        return jax.random.normal(next(ks), shape, f32) * scale

    def gain(shape):
        return 1.0 + 0.02 * jax.random.normal(next(ks), shape, f32)

    L = DEPTH
    n_idx = jnp.arange(SSM_STATE, dtype=f32)
    inp = {}
    inp['x_prompt'] = nrm((BATCH, SEQ, D_MODEL), 1.0)
    inp['x_sample'] = nrm((DEC_BATCH, DEC_SEQ, D_MODEL), 1.0)
    inp['state_ssm_re'] = nrm((L, DEC_BATCH, SSM_GROUPS, SSM_STATE), 0.5)
    inp['state_ssm_im'] = nrm((L, DEC_BATCH, SSM_GROUPS, SSM_STATE), 0.5)
    inp['cache_conv'] = nrm((L, DEC_BATCH, CONV_WIDTH - 1, CONV_DIM), 0.5)
    inp['cache_mem_k'] = nrm((L, DEC_BATCH, N_MEM, MEM_HEADS, MEM_HEAD_DIM), 1.0)
    inp['cache_mem_v'] = nrm((L, DEC_BATCH, N_MEM, MEM_HEADS, MEM_HEAD_DIM), 1.0)
    inp['mem_prompt'] = nrm((BATCH, N_MEM, D_MODEL), 1.0)
    inp['g_mem'] = gain((L, D_MODEL))
    inp['w_mem_k'] = nrm((L, D_MODEL, MEM_HEADS * MEM_HEAD_DIM), D_MODEL ** -0.5)
    inp['w_mem_v'] = nrm((L, D_MODEL, MEM_HEADS * MEM_HEAD_DIM), D_MODEL ** -0.5)
    inp['g_ffn1'] = gain((L, D_MODEL))
    inp['w_ffn1_gate'] = nrm((L, D_MODEL, D_FF), D_MODEL ** -0.5)
    inp['w_ffn1_up'] = nrm((L, D_MODEL, D_FF), D_MODEL ** -0.5)
    inp['w_ffn1_down'] = nrm((L, D_FF, D_MODEL), D_FF ** -0.5)
    inp['g_mix'] = gain((L, D_MODEL))
    inp['w_in'] = nrm((L, D_MODEL, SSM_DIM + 2 * CONV_DIM), D_MODEL ** -0.5)
    inp['ssm_a_re'] = -0.5 + 0.01 * jax.random.normal(next(ks), (L, SSM_GROUPS, SSM_STATE), f32)
    inp['ssm_a_im'] = math.pi * n_idx + 0.01 * jax.random.normal(next(ks), (L, SSM_GROUPS, SSM_STATE), f32)
    inp['ssm_log_dt'] = jax.random.uniform(next(ks), (L, SSM_GROUPS), f32,
                                           minval=math.log(DT_MIN), maxval=math.log(DT_MAX))
    inp['ssm_b_re'] = nrm((L, SSM_GROUPS, SSM_STATE, SSM_GROUP_CH), (2 * SSM_GROUP_CH) ** -0.5)
    inp['ssm_b_im'] = nrm((L, SSM_GROUPS, SSM_STATE, SSM_GROUP_CH), (2 * SSM_GROUP_CH) ** -0.5)
    inp['ssm_c_re'] = nrm((L, SSM_GROUPS, SSM_GROUP_CH, SSM_STATE), (2 * SSM_STATE) ** -0.5)
    inp['ssm_c_im'] = nrm((L, SSM_GROUPS, SSM_GROUP_CH, SSM_STATE), (2 * SSM_STATE) ** -0.5)
    inp['ssm_d'] = nrm((L, SSM_DIM), 1.0)
    inp['w_ssm_glu'] = nrm((L, SSM_DIM, SSM_DIM), SSM_DIM ** -0.5)
    inp['conv_w'] = nrm((L, CONV_WIDTH, CONV_DIM), CONV_WIDTH ** -0.5)
    inp['conv_b'] = nrm((L, CONV_DIM), 0.02)
    inp['conv_ln_g'] = gain((L, CONV_DIM))
    inp['conv_ln_b'] = nrm((L, CONV_DIM), 0.02)
    inp['w_out'] = nrm((L, MIX_DIM, D_MODEL), MIX_DIM ** -0.5)
    inp['g_xattn'] = gain((L, D_MODEL))
    inp['w_mem_q'] = nrm((L, D_MODEL, MEM_HEADS * MEM_HEAD_DIM), D_MODEL ** -0.5)
    inp['w_mem_o'] = nrm((L, MEM_HEADS * MEM_HEAD_DIM, D_MODEL), D_MODEL ** -0.5)
    inp['g_ffn2'] = gain((L, D_MODEL))
    inp['w_ffn2_gate'] = nrm((L, D_MODEL, D_FF), D_MODEL ** -0.5)
    inp['w_ffn2_up'] = nrm((L, D_MODEL, D_FF), D_MODEL ** -0.5)
    inp['w_ffn2_down'] = nrm((L, D_FF, D_MODEL), D_FF ** -0.5)
    inp['g_final'] = gain((D_MODEL,))
    return inp


def reference(x_prompt, x_sample, state_ssm_re, state_ssm_im, cache_conv, cache_mem_k, cache_mem_v,
              mem_prompt, g_mem, w_mem_k, w_mem_v,
              g_ffn1, w_ffn1_gate, w_ffn1_up, w_ffn1_down,
              g_mix, w_in,
              ssm_a_re, ssm_a_im, ssm_log_dt, ssm_b_re, ssm_b_im, ssm_c_re, ssm_c_im, ssm_d, w_ssm_glu,
              conv_w, conv_b, conv_ln_g, conv_ln_b,
              w_out, g_xattn, w_mem_q, w_mem_o,
              g_ffn2, w_ffn2_gate, w_ffn2_up, w_ffn2_down,
              g_final):
    bp = x_prompt.shape[0]
    yp, ys = x_prompt, x_sample
    p_re, p_im, p_conv, p_mk, p_mv = [], [], [], [], []
    s_re, s_im, s_conv = [], [], []
    for l in range(DEPTH):
        lw = (g_ffn1[l], w_ffn1_gate[l], w_ffn1_up[l], w_ffn1_down[l],
              g_mix[l], w_in[l],
              ssm_a_re[l], ssm_a_im[l], ssm_log_dt[l], ssm_b_re[l], ssm_b_im[l],
              ssm_c_re[l], ssm_c_im[l], ssm_d[l], w_ssm_glu[l],
              conv_w[l], conv_b[l], conv_ln_g[l], conv_ln_b[l],
              w_out[l], g_xattn[l], w_mem_q[l], w_mem_o[l],
              g_ffn2[l], w_ffn2_gate[l], w_ffn2_up[l], w_ffn2_down[l])
        mk, mv = mem_kv(mem_prompt, g_mem[l], w_mem_k[l], w_mem_v[l])
        h0 = jnp.zeros((bp, SSM_GROUPS, SSM_STATE), jnp.float32)
        buf0 = jnp.zeros((bp, CONV_WIDTH - 1, CONV_DIM), yp.dtype)
        yp, hr, hi, nb = decoder_layer(yp, mk, mv, h0, h0, buf0, *lw)
        p_re.append(hr); p_im.append(hi); p_conv.append(nb); p_mk.append(mk); p_mv.append(mv)
        ys, hr, hi, nb = decoder_layer(ys, cache_mem_k[l], cache_mem_v[l], state_ssm_re[l],
                                       state_ssm_im[l], cache_conv[l], *lw)
        s_re.append(hr); s_im.append(hi); s_conv.append(nb)
    y_prompt = rms_norm(yp, g_final)
    y_sample = rms_norm(ys, g_final)
    return (y_prompt, y_sample,
            jnp.stack(p_re), jnp.stack(p_im), jnp.stack(p_conv), jnp.stack(p_mk), jnp.stack(p_mv),
            jnp.stack(s_re), jnp.stack(s_im), jnp.stack(s_conv))
```

```python
import contextlib
import math
import numpy as np
import concourse.bass as bass
import concourse.mybir as mybir
from concourse.bass_utils import run_bass_kernel_spmd

F32 = mybir.dt.float32
BF16 = mybir.dt.bfloat16
I32 = mybir.dt.int32
AF = mybir.ActivationFunctionType
ALU = mybir.AluOpType
ENGS = ("pe", "act", "dve", "pool", "sp")
EPS = 1e-6
NT = 2176
TILES = [(0, 512), (512, 512), (1024, 512), (1536, 512), (2048, 128)]
AW = 34800


class Prog:
    def __init__(self, nc):
        self.nc = nc
        self.ops = []

    frozen = False

    def op(self, eng, fn, reads=(), writes=()):
        if self.frozen:
            return
        self.ops.append(dict(eng=eng, fn=fn, r=tuple(reads), w=tuple(writes), dma=False, key=None, bar=False))

    def dma(self, eng, fn, reads=(), writes=(), key=None):
        if self.frozen:
            return
        if key is None:
            key = writes[0] if writes else reads[0]
        self.ops.append(dict(eng=eng, fn=fn, r=tuple(reads), w=tuple(writes), dma=True, key=key, bar=False))

    def barrier(self):
        if self.frozen:
            return
        self.ops.append(dict(bar=True, eng=None, dma=False))

    def emit(self, stack):
        nc = self.nc
        ops = self.ops
        n = len(ops)
        last_w, rd_since = {}, {}
        deps = [set() for _ in range(n)]
        last_on_eng, last_dma_key = {}, {}
        pending = {e: set() for e in ENGS}
        for i, o in enumerate(ops):
            if o["bar"]:
                bd = set(last_on_eng.values()) | set(last_dma_key.values())
                for e in ENGS:
                    pending[e] = set(bd)
                continue
            d = deps[i]
            d |= pending[o["eng"]]
            pending[o["eng"]] = set()
            for r in o["r"]:
                if r in last_w:
                    d.add(last_w[r])
            for w in o["w"]:
                if w in last_w:
                    d.add(last_w[w])
                d.update(rd_since.get(w, ()))
            d.discard(i)
            for r in o["r"]:
                lst = rd_since.setdefault(r, [])
                if not o["dma"]:
                    lst[:] = [j for j in lst if ops[j]["dma"] or ops[j]["eng"] != o["eng"]]
                lst.append(i)
            for w in o["w"]:
                last_w[w] = i
                rd_since[w] = []
            if o["dma"]:
                last_dma_key[o["key"]] = i
            else:
                last_on_eng[o["eng"]] = i
        marked = [False] * n
        for i, o in enumerate(ops):
            if o["bar"]:
                continue
            for j in deps[i]:
                pj = ops[j]
                if pj["dma"]:
                    continue
                if pj["eng"] == o["eng"] == "pe" and not o["dma"]:
                    continue
                marked[j] = True
        eng_cnt = {e: 0 for e in ENGS}
        sigval = [0] * n
        dma_keys = []
        for i, o in enumerate(ops):
            if o["bar"]:
                continue
            if o["dma"]:
                if o["key"] not in dma_keys:
                    dma_keys.append(o["key"])
            elif marked[i]:
                eng_cnt[o["eng"]] += 1
                sigval[i] = eng_cnt[o["eng"]]
        sems = {e: stack.enter_context(nc.semaphore("s_" + e)) for e in ENGS}
        dsems = {k: stack.enter_context(nc.semaphore("d_%d" % i)) for i, k in enumerate(dma_keys)}
        waits = [None] * n
        waited = {e: {} for e in ENGS}
        run_dma = {}
        for i, o in enumerate(ops):
            if o["bar"]:
                continue
            need = {}
            for j in deps[i]:
                pj = ops[j]
                if pj["dma"]:
                    s = ("d", pj["key"])
                    v = 16 * run_dma[pj["key"]]
                else:
                    if pj["eng"] == o["eng"] == "pe" and not o["dma"]:
                        continue
                    s = ("e", pj["eng"])
                    v = sigval[j]
                if need.get(s, 0) < v:
                    need[s] = v
            wl = []
            wd = waited[o["eng"]]
            for s, v in need.items():
                if wd.get(s, 0) < v:
                    wd[s] = v
                    wl.append((s, v))
            waits[i] = wl
            if o["dma"]:
                run_dma[o["key"]] = run_dma.get(o["key"], 0) + 1
        final_dma = dict(run_dma)
        prog = {e: [i for i, o in enumerate(ops) if not o["bar"] and o["eng"] == e] for e in ENGS}
        pc = {e: 0 for e in ENGS}
        val = {}
        progress = True
        while progress:
            progress = False
            for e in ENGS:
                while pc[e] < len(prog[e]):
                    i = prog[e][pc[e]]
                    if all(val.get(s_, 0) >= v for s_, v in waits[i]):
                        o = ops[i]
                        if o["dma"]:
                            val[("d", o["key"])] = val.get(("d", o["key"]), 0) + 16
                        elif marked[i]:
                            val[("e", e)] = val.get(("e", e), 0) + 1
                        pc[e] += 1
                        progress = True
                    else:
                        break
        stuck = {e: (pc[e], len(prog[e])) for e in ENGS if pc[e] < len(prog[e])}
        if stuck:
            for e in stuck:
                i = prog[e][pc[e]]
                print("DEADLOCK", e, "op", i, "waits", waits[i], "have", {s_: val.get(s_, 0) for s_, _ in waits[i]}, ops[i]["r"], ops[i]["w"])
            raise RuntimeError("semaphore deadlock in recorded program: %s" % stuck)
        self.stats = dict(n_ops=n, n_waits=sum(len(w) for w in waits if w), n_marked=sum(marked),
                          n_dma_sems=len(dma_keys), per_eng={e: sum(1 for o in ops if o["eng"] == e) for e in ENGS})
        block = stack.enter_context(nc.Block())

        def run_engine(ename, e):
            for i, o in enumerate(ops):
                if o["eng"] != ename:
                    continue
                for (s, v) in waits[i]:
                    e.wait_ge(dsems[s[1]] if s[0] == "d" else sems[s[1]], v)
                ins = o["fn"](e)
                if o["dma"]:
                    ins.then_inc(dsems[o["key"]], 16)
                elif marked[i]:
                    ins.then_inc(sems[ename], 1)
            if ename == "sp":
                for k, c in final_dma.items():
                    e.wait_ge(dsems[k], 16 * c)
                for en in ENGS:
                    if en != "sp" and eng_cnt[en] > 0:
                        e.wait_ge(sems[en], eng_cnt[en])

        block.tensor(lambda e: run_engine("pe", e))
        block.scalar(lambda e: run_engine("act", e))
        block.vector(lambda e: run_engine("dve", e))
        block.gpsimd(lambda e: run_engine("pool", e))
        block.sync(lambda e: run_engine("sp", e))


def build_nc(dbg=(), upto=99):
    nc = bass.Bass("TRN2", target_bir_lowering=False)

    def din(name, shape):
        return nc.dram_tensor(name, list(shape), F32, kind="ExternalInput").ap()

    def dout(name, shape):
        return nc.dram_tensor(name, list(shape), F32, kind="ExternalOutput").ap()

    xp = din("xp", [2048, 1024]); xs = din("xs", [128, 1024])
    st_re = din("st_re", [16, 2048]); st_im = din("st_im", [16, 2048])
    cconv = din("cconv", [16, 30, 512])
    ck = din("ck", [16, 256, 1024]); cv = din("cv", [16, 256, 1024])
    memp = din("memp", [256, 1024])
    vecs = din("vecs", [64, 1024])
    w_mem_k = din("w_mem_k", [1024, 1024]); w_mem_v = din("w_mem_v", [1024, 1024])
    w_gate1 = din("w_gate1", [1024, 2816]); w_up1 = din("w_up1", [1024, 2816]); w_down1 = din("w_down1", [2816, 1024])
    w_in = din("w_in", [1024, 1536])
    a_re = din("a_re", [32, 64]); a_im = din("a_im", [32, 64]); log_dt = din("log_dt", [1, 32])
    b_re = din("b_re", [32, 64, 16]); b_im = din("b_im", [32, 64, 16])
    c_re = din("c_re", [32, 16, 64]); c_im = din("c_im", [32, 16, 64])
    w_glu = din("w_glu", [512, 512]); w_out = din("w_out", [1024, 1024])
    w_q = din("w_q", [1024, 1024]); w_o = din("w_o", [1024, 1024])
    w_gate2 = din("w_gate2", [1024, 2816]); w_up2 = din("w_up2", [1024, 2816]); w_down2 = din("w_down2", [2816, 1024])
    ident_d = din("ident", [128, 128])

    y_p = dout("y_p", [2048, 1024]); y_s = dout("y_s", [128, 1024])
    nre_p = dout("nre_p", [16, 128]); nim_p = dout("nim_p", [16, 128])
    nconv_p = dout("nconv_p", [30, 512])
    nk_p = dout("nk_p", [256, 1024]); nv_p = dout("nv_p", [256, 1024])
    nre_s = dout("nre_s", [16, 2048]); nim_s = dout("nim_s", [16, 2048])
    nconv_s = dout("nconv_s", [16, 30, 512])
    dbg_out = {k: dout("dbg_" + k, [128, 8, NT]) for k in dbg if k.startswith("x")}
    dumps = []

    def dump(name, ap, dt=F32):
        if "s5" not in dbg:
            return
        shp = [int(x) for x in ap.shape]
        d = nc.dram_tensor("dmp_" + name, shp, dt, kind="ExternalOutput").ap()
        P.barrier()
        P.dma("sp", lambda e: e.dma_start(out=d, in_=ap), (), (), key="dmp_" + name)
        P.barrier()

    P = Prog(nc)

    def chk(k):
        if k >= upto:
            P.frozen = True

    st = contextlib.ExitStack()
    with st:
        st.enter_context(nc.allow_non_contiguous_dma(reason="small strided parameter loads"))
        xT = st.enter_context(nc.sbuf_tensor("xT", [128, 8, NT], F32))
        arena = st.enter_context(nc.sbuf_tensor("arena", [128, AW], F32))
        ident = st.enter_context(nc.sbuf_tensor("identf", [128, 128], F32))
        identb = st.enter_context(nc.sbuf_tensor("identb", [128, 128], BF16))
        onesb = st.enter_context(nc.sbuf_tensor("onesb", [128, 128], BF16))
        ones512 = st.enter_context(nc.sbuf_tensor("ones512", [128, 128], BF16))
        PVT = st.enter_context(nc.sbuf_tensor("PVT", [128, 8, 64], F32))
        banks = [st.enter_context(nc.psum_tensor("bank%d" % i, [128, 512], F32)) for i in range(8)]

        def bank(i):
            return banks[i][:, :]

        def bankb(i):
            return banks[i][:, :].bitcast(BF16)

        class Bump:
            def __init__(self, base=0):
                self.off = base

            def alloc(self, shape, dt):
                nel = int(np.prod(shape))
                words = (nel * (4 if dt in (F32, I32) else 2) + 3) // 4
                v = arena[:, self.off:self.off + words]
                self.off += words
                assert self.off <= AW, ("arena overflow", self.off)
                if dt != F32:
                    v = v.bitcast(dt)
                    v = v[:, 0:nel]
                if len(shape) > 1:
                    names = ["d%d" % i for i in range(len(shape))]
                    pat = "p (" + " ".join(names) + ") -> p " + " ".join(names)
                    v = v.rearrange(pat, **{nm: s for nm, s in zip(names[1:], shape[1:])})
                return v

        def mm(out, lhsT, rhs, start, stop, reads, writes, **kw):
            P.op("pe", lambda e: e.matmul(out, lhsT=lhsT, rhs=rhs, start=start, stop=stop, **kw), reads, writes)

        def tr(out, in_, idn, reads, writes):
            P.op("pe", lambda e: e.transpose(out=out, in_=in_, identity=idn), reads, writes)

        def act(out, in_, func, reads, writes, **kw):
            P.op("act", lambda e: e.activation(out=out, in_=in_, func=func, **kw), reads, writes)

        def tt(out, in0, in1, op, reads, writes, eng="dve"):
            P.op(eng, lambda e: e.tensor_tensor(out=out, in0=in0, in1=in1, op=op), reads, writes)

        def ts(out, in0, s1, s2, op0, op1, reads, writes, eng="dve"):
            P.op(eng, lambda e: e.tensor_scalar(out=out, in0=in0, scalar1=s1, scalar2=s2, op0=op0, op1=op1), reads, writes)

        def ts1(out, in0, s1, op0, reads, writes, eng="dve"):
            P.op(eng, lambda e: e.tensor_scalar(out=out, in0=in0, scalar1=s1, scalar2=None, op0=op0), reads, writes)

        def stt(out, in0, scalar, in1, op0, op1, reads, writes):
            P.op("dve", lambda e: e.scalar_tensor_tensor(out=out, in0=in0, scalar=scalar, in1=in1, op0=op0, op1=op1), reads, writes)

        def cp(out, in_, reads, writes, eng="dve"):
            if eng == "act":
                P.op("act", lambda e: e.activation(out=out, in_=in_, func=AF.Copy), reads, writes)
            else:
                P.op(eng, lambda e: e.tensor_copy(out=out, in_=in_), reads, writes)

        def memset(ap, val, writes, eng="dve"):
            P.op(eng, lambda e: e.memset(ap, val), (), writes)

        def recip(out, in_, reads, writes):
            P.op("dve", lambda e: e.reciprocal(out=out, in_=in_), reads, writes)

        def ld(out, in_, writes, eng="sp", key=None):
            P.dma(eng, lambda e: e.dma_start(out=out, in_=in_), (), writes, key=key)

        def stor(out, in_, reads, key=None):
            P.dma("sp", lambda e: e.dma_start(out=out, in_=in_), reads, (), key=key)

        def wview(w):
            return w.rearrange("(kc p) f -> p kc f", p=128)

        def gcol(r, kc):
            return PVT[:, kc, r:r + 1]

        alt = [0]

        act_only = [False]

        def evac(out, in_, reads, writes):
            alt[0] ^= 1
            cp(out, in_, reads, writes, eng=("act" if (alt[0] or act_only[0]) else "dve"))

        b0 = Bump()
        vst = b0.alloc([1024], F32)
        xst = [b0.alloc([1024], F32) for _ in range(3)]
        ld(ident[:, :], ident_d, ["ident"])
        ld(vst[0:64, :], vecs, ["vst"])
        cp(identb[:, :], ident[:, :], ["ident"], ["identb"])
        memset(onesb[:, :], 1.0 / 1024.0, ["onesb"])
        memset(ones512[:, :], 1.0 / 512.0, ["ones512"])
        for kc in range(8):
            tr(bank(7)[:, 64 * kc:64 * kc + 64], vst[0:64, 128 * kc:128 * kc + 128], ident[0:64, 0:64],
               ["vst", "ident"], ["b7"])
        cp(PVT[:, :, :], bank(7).rearrange("p (k r) -> p k r", r=64), ["b7"], ["PVT"])
        for i in range(17):
            src = xp[128 * i:128 * i + 128, :] if i < 16 else xs
            slot = i % 3
            ld(xst[slot][:, :], src, ["xst%d" % slot])
            for h in range(2):
                bk = 2 * (i % 2) + h
                for k4 in range(4):
                    kc = 4 * h + k4
                    tr(bank(bk)[:, 128 * k4:128 * k4 + 128], xst[slot][:, 128 * kc:128 * kc + 128], ident[:, :],
                       ["xst%d" % slot, "ident"], ["b%d" % bk])
                evac(xT[:, 4 * h:4 * h + 4, 128 * i:128 * i + 128], bank(bk).rearrange("p (k t) -> p k t", t=128),
                     ["b%d" % bk], ["x%d" % (i // 4)])
        P.barrier()

        chk(0)
        act_only[0] = True
        def norm_tile(t, grow, dst, dst_key, sq, rstd, psb, fp32_out=False):
            c0, N = TILES[t]
            xk = "x%d" % t
            act(sq[:, :, 0:N], xT[:, :, c0:c0 + N], AF.Square, [xk], ["sq"])
            for kc in range(8):
                mm(bank(psb)[:, 0:N], onesb[:, :], sq[:, kc, 0:N], kc == 0, kc == 7, ["onesb", "sq"], ["b%d" % psb])
            act(rstd[:, 0:N], bank(psb)[:, 0:N], AF.Sqrt, ["b%d" % psb], ["rstd"], bias=EPS, scale=1.0)
            recip(rstd[:, 0:N], rstd[:, 0:N], ["rstd"], ["rstd"])
            for kc in range(8):
                stt(dst(kc, N), xT[:, kc, c0:c0 + N], gcol(grow, kc), rstd[:, 0:N], ALU.mult, ALU.mult,
                    [xk, "PVT", "rstd"], [dst_key])

        win_fixed = Bump(28000).alloc([8, 1536], BF16)

        def ffn(grow, wgate, wup, wdown, tag, tail_cb=None, extra=None):
            bp = Bump()
            xn = bp.alloc([8, NT], BF16)
            wgs = [bp.alloc([8, 512], BF16) for _ in range(2)]
            wus = [bp.alloc([8, 512], BF16) for _ in range(2)]
            wds = [bp.alloc([4, 1024], BF16) for _ in range(2)]
            hT = [bp.alloc([4, 512], BF16) for _ in range(2)]
            sg = [bp.alloc([512], F32) for _ in range(2)]
            sq = bp.alloc([8, 512], BF16)
            rstd = bp.alloc([512], F32)
            groups = [(0, 2), (2, 4), (6, 4), (10, 4), (14, 4), (18, 4)]

            def load_w(gi):
                s = gi % 2
                f0, G = groups[gi]
                ld(wgs[s][:, :, 0:128 * G], wview(wgate)[:, :, 128 * f0:128 * (f0 + G)], ["wg%d" % s], eng="pool")
                ld(wus[s][:, :, 0:128 * G], wview(wup)[:, :, 128 * f0:128 * (f0 + G)], ["wu%d" % s], eng="pool")
                ld(wds[s][:, 0:G, :], wdown[128 * f0:128 * (f0 + G), :].rearrange("(g p) o -> p g o", p=128),
                   ["wd%d" % s], eng="pool")

            load_w(0)
            load_w(1)
            if tag == "f1":
                ld(win_fixed[:, :, :], wview(w_in), ["win"], eng="pool")
            if extra is not None:
                extra(bp, sq, rstd)
            def do_norm(t):
                c0 = TILES[t][0]
                norm_tile(t, grow, lambda kc, N, c0=c0: xn[:, kc, c0:c0 + N], "xn%d" % t, sq, rstd, 6)

            do_norm(0)
            do_norm(1)
            cnt = [0, 0, 0]

            def GU(gi, t):
                s = gi % 2
                c0, N = TILES[t]
                hs = cnt[1] % 2
                for g in range(groups[gi][1]):
                    a = cnt[0] % 2
                    cnt[0] += 1
                    for kc in range(8):
                        mm(bank(a)[:, 0:N], wgs[s][:, kc, 128 * g:128 * g + 128], xn[:, kc, c0:c0 + N], kc == 0, kc == 7,
                           ["wg%d" % s, "xn%d" % t], ["b%d" % a])
                    for kc in range(8):
                        mm(bank(2 + a)[:, 0:N], wus[s][:, kc, 128 * g:128 * g + 128], xn[:, kc, c0:c0 + N], kc == 0, kc == 7,
                           ["wu%d" % s, "xn%d" % t], ["b%d" % (2 + a)])
                    act(sg[a][:, 0:N], bank(a)[:, 0:N], AF.Silu, ["b%d" % a], ["sg%d" % a])
                    tt(hT[hs][:, g, 0:N], sg[a][:, 0:N], bank(2 + a)[:, 0:N], ALU.mult,
                       ["sg%d" % a, "b%d" % (2 + a)], ["hT%d" % hs])
                cnt[1] += 1
                return hs

            def D(gi, t, hs):
                s = gi % 2
                c0, N = TILES[t]
                G = groups[gi][1]
                for oc in range(8):
                    a = 4 + cnt[2] % 2
                    cnt[2] += 1
                    for g in range(G):
                        mm(bank(a)[:, 0:N], wds[s][:, g, 128 * oc:128 * oc + 128], hT[hs][:, g, 0:N], g == 0, g == G - 1,
                           ["wd%d" % s, "hT%d" % hs], ["b%d" % a])
                    stt(xT[:, oc, c0:c0 + N], bank(a)[:, 0:N], 0.5, xT[:, oc, c0:c0 + N], ALU.mult, ALU.add,
                        ["b%d" % a, "x%d" % t], ["x%d" % t])

            prev = None
            for gi in range(6):
                for t in range(5):
                    hs = GU(gi, t)
                    if gi == 0 and t + 2 < 5:
                        do_norm(t + 2)
                    if prev is not None:
                        D(*prev)
                        if prev[1] == 4 and prev[0] + 2 < 6:
                            load_w(prev[0] + 2)
                        if prev[0] == 5 and tail_cb is not None:
                            tail_cb(prev[1])
                    prev = (gi, t, hs)
            D(*prev)
            if tail_cb is not None:
                tail_cb(prev[1])
                tail_cb(None)
            P.barrier()

        ffn(0, w_gate1, w_up1, w_down1, "f1")
        if "x1" in dbg:
            stor(dbg_out["x1"], xT[:, :, :], ["x0", "x1", "x2", "x3", "x4"], key="dbgx1")
            P.barrier()

        chk(1)
        pm = Bump()
        ub = pm.alloc([4, NT], BF16)
        base_u = pm.off
        VB = pm.alloc([4, 30 + 2048], BF16)
        VS = pm.alloc([4, 16, 38], BF16)
        vtail = pm.alloc([4, 30], F32)
        vnew = pm.alloc([4, 128], F32)
        base_m = pm.off

        DG = Bump(19100).alloc([4, 31, 128], BF16)
        assert 19100 + 7936 <= 28000
        bp = Bump(base_m)
        win = win_fixed
        xnt2 = [bp.alloc([8, 512], BF16) for _ in range(2)]
        sq = bp.alloc([8, 512], BF16)
        rstd = bp.alloc([512], F32)
        sgm = [bp.alloc([512], F32) for _ in range(2)]
        cst = [bp.alloc([512], F32) for _ in range(2)]
        memset(VB[:, :, 0:30], 0.0, ["VB"])
        for i in range(4):
            s = i % 2
            ld(cst[s][0:120, :], cconv[4 * i:4 * i + 4, :, :].rearrange("s t c -> (s t) c"), ["cst%d" % s])
            for q in range(4):
                tr(bank(7)[:, 120 * q:120 * q + 120], cst[s][0:120, 128 * q:128 * q + 128], ident[0:120, 0:120],
                   ["cst%d" % s, "ident"], ["b7"])
            cp(VS[:, :, 4 * i:4 * i + 4, 0:30], bank(7)[:, 0:480].rearrange("p (q s t) -> p q s t", q=4, s=4),
               ["b7"], ["VS"], eng="act")
        P.dma("sp", lambda e: e.dma_start(out=nconv_s[:, 0:22, :], in_=cconv[:, 8:30, :]), (), (), key="nconv_copy")
        k1 = [0]

        def m1_norm(t):
            xb = xnt2[t % 2]
            norm_tile(t, 1, lambda kc, N, xb=xb: xb[:, kc, 0:N], "xnt%d" % (t % 2), sq, rstd, 6)

        m1_norm(0)
        for t in range(5):
            c0, N = TILES[t]
            xnt = xnt2[t % 2]
            xk_ = "xnt%d" % (t % 2)
            if t + 1 < 5:
                m1_norm(t + 1)
            dgq = list(range(31)) if t < 4 else []

            def dg_some(n):
                for _ in range(n):
                    if dgq:
                        k = dgq.pop(0)
                        act(DG[:, t, k, :], identb[:, :], AF.Identity, ["identb", "PVT"], ["DG%d" % t], scale=gcol(8 + k, t))
            for q in range(4):
                a = k1[0] % 2
                k1[0] += 1
                for kc in range(8):
                    mm(bank(a)[:, 0:N], win[:, kc, 128 * q:128 * q + 128], xnt[:, kc, 0:N], kc == 0, kc == 7,
                       ["win", xk_], ["b%d" % a])
                cp(ub[:, q, c0:c0 + N], bank(a)[:, 0:N], ["b%d" % a], ["ub%d" % t], eng="act")
                dg_some(4)
            for q in range(4):
                a = k1[0] % 2
                k1[0] += 1
                for kc in range(8):
                    mm(bank(2 + a)[:, 0:N], win[:, kc, 512 + 128 * q:512 + 128 * q + 128], xnt[:, kc, 0:N], kc == 0, kc == 7,
                       ["win", xk_], ["b%d" % (2 + a)])
                for kc in range(8):
                    mm(bank(4 + a)[:, 0:N], win[:, kc, 1024 + 128 * q:1024 + 128 * q + 128], xnt[:, kc, 0:N], kc == 0, kc == 7,
                       ["win", xk_], ["b%d" % (4 + a)])
                act(sgm[a][:, 0:N], bank(4 + a)[:, 0:N], AF.Sigmoid, ["b%d" % (4 + a)], ["sgm%d" % a])
                if t < 4:
                    tt(VB[:, q, 30 + c0:30 + c0 + N], bank(2 + a)[:, 0:N], sgm[a][:, 0:N], ALU.mult,
                       ["b%d" % (2 + a), "sgm%d" % a], ["VB"])
                    if t == 3:
                        tt(vtail[:, q, :], bank(2 + a)[:, 482:512], sgm[a][:, 482:512], ALU.mult,
                           ["b%d" % (2 + a), "sgm%d" % a], ["vtail"])
                else:
                    tt(vnew[:, q, :], bank(2 + a)[:, 0:N], sgm[a][:, 0:N], ALU.mult,
                       ["b%d" % (2 + a), "sgm%d" % a], ["vnew"])
                    cp(VS[:, q, :, 30:38], vnew[:, q, :].rearrange("p (s t) -> p s t", t=8), ["vnew"], ["VS"])
                dg_some(4)
            dg_some(31)
        P.barrier()

        chk(2)
        pf = Bump(32000)
        Bre = pf.alloc([16, 32], F32); Bim = pf.alloc([16, 32], F32)
        ZnR = pf.alloc([4, 128], F32); ZnI = pf.alloc([4, 128], F32)
        LR = pf.alloc([16], F32); LI = pf.alloc([16], F32); DT = pf.alloc([16], F32)
        AR = pf.alloc([16], F32); AI = pf.alloc([16], F32)
        tA = pf.alloc([16], F32); tB = pf.alloc([16], F32); tC = pf.alloc([16], F32); tI = pf.alloc([16], I32)
        CRE = pf.alloc([16], F32); CIM = pf.alloc([16], F32); DECs = pf.alloc([16], F32)
        PWr = pf.alloc([9, 16], F32); PWi = pf.alloc([9, 16], F32)
        assert pf.off <= AW
        S5K = ["s5t"]
        chain_ops = []

        def record_s5_chain():
            saved = P.ops
            P.ops = []
            act(DT[:, :], DT[:, :], AF.Exp, ["DT"], S5K)
            tt(tA[:, :], LR[:, :], DT[:, :], ALU.mult, ["LR"] + S5K, S5K)
            act(tB[:, :], tA[:, :], AF.Exp, S5K, S5K)
            act(DECs[:, :], tA[:, :], AF.Exp, S5K, S5K, scale=8.0)
            tt(tA[:, :], LI[:, :], DT[:, :], ALU.mult, ["LI"] + S5K, S5K)

            def sincos(dst, shift):
                TWO_PI = 2.0 * math.pi
                ts(tC[:, :], tA[:, :], 8.0 * math.pi + shift, 1.0 / TWO_PI, ALU.add, ALU.mult, S5K, S5K)
                cp(tI[:, :], tC[:, :], S5K, S5K)
                cp(dst, tI[:, :], S5K, S5K)
                tt(dst, tC[:, :], dst, ALU.subtract, S5K, S5K)
                ts(tC[:, :], dst, 0.5, -1.0, ALU.is_gt, ALU.mult, S5K, S5K)
                tt(dst, dst, tC[:, :], ALU.add, S5K, S5K)
                ts(dst, dst, TWO_PI, math.pi, ALU.mult, ALU.min, S5K, S5K)
                ts1(dst, dst, -math.pi, ALU.max, S5K, S5K)
                act(dst, dst, AF.Sin, S5K, S5K)

            sincos(AI[:, :], 0.0)
            sincos(AR[:, :], 0.5 * math.pi)
            tt(AR[:, :], AR[:, :], tB[:, :], ALU.mult, S5K, S5K)
            tt(AI[:, :], AI[:, :], tB[:, :], ALU.mult, S5K, S5K)
            tt(tA[:, :], LR[:, :], LR[:, :], ALU.mult, ["LR"] + S5K, S5K)
            tt(tB[:, :], LI[:, :], LI[:, :], ALU.mult, ["LI"] + S5K, S5K)
            tt(tA[:, :], tA[:, :], tB[:, :], ALU.add, S5K, S5K)
            recip(tA[:, :], tA[:, :], S5K, S5K)
            ts1(tB[:, :], AR[:, :], -1.0, ALU.add, S5K, S5K)
            tt(CRE[:, :], tB[:, :], LR[:, :], ALU.mult, S5K, S5K)
            tt(tC[:, :], AI[:, :], LI[:, :], ALU.mult, S5K, S5K)
            tt(CRE[:, :], CRE[:, :], tC[:, :], ALU.add, S5K, S5K)
            tt(CRE[:, :], CRE[:, :], tA[:, :], ALU.mult, S5K, S5K)
            tt(CIM[:, :], AI[:, :], LR[:, :], ALU.mult, S5K, S5K)
            tt(tC[:, :], tB[:, :], LI[:, :], ALU.mult, S5K, S5K)
            tt(CIM[:, :], CIM[:, :], tC[:, :], ALU.subtract, S5K, S5K)
            tt(CIM[:, :], CIM[:, :], tA[:, :], ALU.mult, S5K, S5K)
            memset(PWr[:, 0, :], 1.0, S5K); memset(PWi[:, 0, :], 0.0, S5K)
            cp(PWr[:, 1, :], AR[:, :], S5K, S5K); cp(PWi[:, 1, :], AI[:, :], S5K, S5K)
            for k in range(1, 8):
                tt(tA[:, :], PWr[:, k, :], AR[:, :], ALU.mult, S5K, S5K)
                tt(tB[:, :], PWi[:, k, :], AI[:, :], ALU.mult, S5K, S5K)
                tt(PWr[:, k + 1, :], tA[:, :], tB[:, :], ALU.subtract, S5K, S5K)
                tt(tA[:, :], PWr[:, k, :], AI[:, :], ALU.mult, S5K, S5K)
                tt(tB[:, :], PWi[:, k, :], AR[:, :], ALU.mult, S5K, S5K)
                tt(PWi[:, k + 1, :], tA[:, :], tB[:, :], ALU.add, S5K, S5K)

            chain_ops.extend(P.ops)
            P.ops = saved

        def sprinkle(n):
            P.ops.extend(chain_ops[:n])
            del chain_ops[:n]

        def s5_prefetch():
            for e_ in range(2):
                lo, hi = 64 * e_, 64 * e_ + 64
                ld(LR[lo:hi, :], a_re.rearrange("(P e) n -> e n P", e=2)[e_], ["LR"])
                ld(LI[lo:hi, :], a_im.rearrange("(P e) n -> e n P", e=2)[e_], ["LI"])
                ld(DT[lo:hi, :], log_dt.rearrange("o (P e) -> e (o P)", e=2)[e_:e_ + 1, :].to_broadcast([64, 16]), ["DT"])
            memset(Bre[:, :, :], 0.0, ["Bre"]); memset(Bim[:, :, :], 0.0, ["Bim"])
            for e_ in range(2):
                lo, hi = 64 * e_, 64 * e_ + 64
                ld(Bre[lo:hi, :, 16 * e_:16 * e_ + 16], b_re.rearrange("(P e) n c -> e n P c", e=2)[e_], ["Bre"])
                ld(Bim[lo:hi, :, 16 * e_:16 * e_ + 16], b_im.rearrange("(P e) n c -> e n P c", e=2)[e_], ["Bim"])
            for (csrc, Zn_, nm) in ((c_re, ZnR, "ZnR"), (c_im, ZnI, "ZnI")):
                for d_ in range(2):
                    ld(Zn_[:, :, 64 * d_:64 * d_ + 64], csrc.rearrange("(t g) c n -> (g c) t n", g=8), [nm])

        bp = Bump(base_m)
        wol = bp.alloc([4, 1024], BF16)
        yc = bp.alloc([4, 512], F32)
        ycb = bp.alloc([4, 512], BF16)
        sqb = bp.alloc([4, 512], BF16)
        m2 = bp.alloc([512], F32)
        rs = bp.alloc([512], F32)
        nmr = bp.alloc([512], F32)
        assert bp.off <= 19100, bp.off
        bp = Bump(19100 + 7936)
        tq = [bp.alloc([512], F32) for _ in range(2)]
        sgq = [bp.alloc([512], F32) for _ in range(2)]
        yn_ = [bp.alloc([512], F32) for _ in range(2)]
        cout = bp.alloc([4, 512], BF16)
        ost = bp.alloc([512], F32)
        ld(wol[:, :, :], w_out[512:1024, :].rearrange("(kc p) f -> p kc f", p=128), ["wol"], eng="pool")
        for q in range(4):
            tr(bank(7)[0:30, 128 * q:128 * q + 128], vtail[:, q, :], ident[:, :], ["vtail", "ident"], ["b7"])
        cp(ost[0:30, :], bank(7)[0:30, :], ["b7"], ["ost"])
        stor(nconv_p, ost[0:30, :], ["ost"])
        for q in range(4):
            tr(bank(6)[:, 128 * q:128 * q + 128], vnew[:, q, :], ident[:, :], ["vnew", "ident"], ["b6"])
        cp(yc[:, 0, :], bank(6)[:, :], ["b6"], ["yc0"])
        for s in range(16):
            stor(nconv_s[s, 22:30, :], yc[8 * s:8 * s + 8, 0, :], ["yc0"], key="ycst")

        def conv_stage(t):
            c0, N = TILES[t]
            for q in range(4):
                for k in range(31):
                    if t < 4:
                        rhs = VB[:, q, c0 + k:c0 + k + 512]
                    else:
                        rhs = VS[:, q, :, k:k + 8]
                    mm(bank(q)[:, 0:N], DG[:, q, k, :], rhs, k == 0, k == 30, ["DG%d" % q, "VB", "VS"], ["b%d" % q])
                act(yc[:, q, 0:N], bank(q)[:, 0:N], AF.Identity, ["b%d" % q, "PVT"], ["yc%d" % q], bias=gcol(6, 4 + q), scale=1.0)
                cp(ycb[:, q, 0:N], yc[:, q, 0:N], ["yc%d" % q], ["ycb"], eng="act")
                act(sqb[:, q, 0:N], yc[:, q, 0:N], AF.Square, ["yc%d" % q], ["sqb"])

        def ln_stage(t):
            c0, N = TILES[t]
            for q in range(4):
                mm(bank(4)[:, 0:N], ones512[:, :], ycb[:, q, 0:N], q == 0, q == 3, ["ones512", "ycb"], ["b4"])
            for q in range(4):
                mm(bank(5)[:, 0:N], ones512[:, :], sqb[:, q, 0:N], q == 0, q == 3, ["ones512", "sqb"], ["b5"])
            act(m2[:, 0:N], bank(4)[:, 0:N], AF.Square, ["b4"], ["m2"])
            tt(rs[:, 0:N], bank(5)[:, 0:N], m2[:, 0:N], ALU.subtract, ["b5", "m2"], ["rs"])
            ts(rs[:, 0:N], rs[:, 0:N], 0.0, EPS, ALU.max, ALU.add, ["rs"], ["rs"])
            act(rs[:, 0:N], rs[:, 0:N], AF.Sqrt, ["rs"], ["rs"])
            recip(rs[:, 0:N], rs[:, 0:N], ["rs"], ["rs"])
            stt(nmr[:, 0:N], bank(4)[:, 0:N], -1.0, rs[:, 0:N], ALU.mult, ALU.mult, ["b4", "rs"], ["nmr"])
            for q in range(4):
                a = q % 2
                tt(tq[a][:, 0:N], yc[:, q, 0:N], rs[:, 0:N], ALU.mult, ["yc%d" % q, "rs"], ["tq%d" % a])
                tt(tq[a][:, 0:N], tq[a][:, 0:N], nmr[:, 0:N], ALU.add, ["tq%d" % a, "nmr"], ["tq%d" % a])
                act(sgq[a][:, 0:N], tq[a][:, 0:N], AF.Sigmoid, ["tq%d" % a, "PVT"], ["sgq%d" % a], scale=gcol(7, q), bias=gcol(7, 4 + q))
                act(yn_[a][:, 0:N], tq[a][:, 0:N], AF.Identity, ["tq%d" % a, "PVT"], ["yn%d" % a], scale=gcol(7, q), bias=gcol(7, 4 + q))
                tt(cout[:, q, 0:N], yn_[a][:, 0:N], sgq[a][:, 0:N], ALU.mult, ["yn%d" % a, "sgq%d" % a], ["cout"])
                if t >= 1:
                    sprinkle(4)

        def wout_lo_stage(t):
            c0, N = TILES[t]
            for oc in range(8):
                a = 6 + oc % 2
                for kc in range(4):
                    mm(bank(a)[:, 0:N], wol[:, kc, 128 * oc:128 * oc + 128], cout[:, kc, 0:N], kc == 0, kc == 3,
                       ["wol", "cout"], ["b%d" % a])
                tt(xT[:, oc, c0:c0 + N], bank(a)[:, 0:N], xT[:, oc, c0:c0 + N], ALU.add, ["b%d" % a, "x%d" % t], ["x%d" % t])
                if t >= 1:
                    sprinkle(2)

        conv_stage(0)
        s5_prefetch()
        record_s5_chain()
        for t in range(5):
            ln_stage(t)
            if t + 1 < 5:
                conv_stage(t + 1)
            wout_lo_stage(t)
        sprinkle(len(chain_ops))
        P.barrier()
        chk(3)
        bp = Bump(base_u)
        KL = bp.alloc([4, 8, 128], BF16)
        WXS = bp.alloc([4, 8, 2, 128], BF16)
        WY = bp.alloc([16, 8, 2, 32], BF16)
        wglu = bp.alloc([4, 512], BF16)
        woh = bp.alloc([4, 1024], BF16)
        T1 = bp.alloc([2, 16], F32)
        T2 = bp.alloc([2, 16], F32)
        DEC = bp.alloc([16], F32)
        C64 = bp.alloc([16, 64], F32)
        S64 = bp.alloc([16, 64], F32)
        base_rt = bp.off
        ld(wglu[:, :, :], w_glu.rearrange("(kc p) f -> p kc f", p=128), ["wglu"], eng="pool")
        ld(woh[:, :, :], w_out[0:512, :].rearrange("(kc p) f -> p kc f", p=128), ["woh"], eng="pool")
        BBr = bp.alloc([16, 32], F32); BBi = bp.alloc([16, 32], F32)
        Cre = bp.alloc([16, 32], F32); Cim = bp.alloc([16, 32], F32); CimN = bp.alloc([16, 32], F32)
        Xr = bp.alloc([16, 32], F32); Xi = bp.alloc([16, 32], F32)
        Xr2 = bp.alloc([16, 32], F32); Xi2 = bp.alloc([16, 32], F32)
        Xb = [[bp.alloc([16, 32], BF16) for _ in range(2)] for _ in range(2)]
        Cbr = bp.alloc([16, 32], BF16); Cbi = bp.alloc([16, 32], BF16)
        u1 = bp.alloc([16, 32], F32); u2 = bp.alloc([16, 32], F32); u3 = bp.alloc([16, 32], F32); u4 = bp.alloc([16, 32], F32)
        ucnt = [0]
        memset(Cre[:, :, :], 0.0, ["Cre"]); memset(Cim[:, :, :], 0.0, ["Cim"])
        for (Zn, znk, cdst, nm) in ((ZnR, "ZnR", Cre, "Cre"), (ZnI, "ZnI", Cim, "Cim")):
            for T_ in range(4):
                tr(bank(7)[:, 128 * T_:128 * T_ + 128], Zn[:, T_, :], ident[:, :], [znk, "ident"], ["b7"])
            for e_ in range(2):
                lo, hi = 64 * e_, 64 * e_ + 64
                cp(cdst[lo:hi, :, 16 * e_:16 * e_ + 16].rearrange("p (t l) c -> p t l c", t=4),
                   bank(7)[lo:hi, :].rearrange("p (t l e c) -> p t l e c", t=4, l=4, e=2)[:, :, :, e_, :], ["b7"], [nm], eng="act")
        act(CimN[:, :, :], Cim[:, :, :], AF.Copy, ["Cim"], ["CimN"], scale=-1.0)
        cp(Cbr[:, :, :], Cre[:, :, :], ["Cre"], ["Cbr"], eng="act")
        cp(Cbi[:, :, :], CimN[:, :, :], ["CimN"], ["Cbi"], eng="act")
        cp(DEC[:, :], DECs[:, :], S5K, ["DEC"])
        cp(T1[:, 0, :], PWr[:, 8, :], S5K, ["T12"]); cp(T1[:, 1, :], PWr[:, 8, :], S5K, ["T12"])
        ts1(T2[:, 0, :], PWi[:, 8, :], -1.0, ALU.mult, S5K, ["T12"]); cp(T2[:, 1, :], PWi[:, 8, :], S5K, ["T12"])

        Ec = bp.alloc([16], F32); Es = bp.alloc([16], F32); r1 = bp.alloc([16, 32], F32); r2 = bp.alloc([16, 32], F32)
        recip(tA[:, :], DEC[:, :], ["DEC"], S5K)
        tt(Ec[:, :], PWr[:, 8, :], tA[:, :], ALU.mult, S5K, ["E"])
        tt(Es[:, :], PWi[:, 8, :], tA[:, :], ALU.mult, S5K, ["E"])
        memset(C64[:, :, 0:1], 1.0, ["CS"]); memset(S64[:, :, 0:1], 0.0, ["CS"])
        m_ = 1
        while m_ < 64:
            bcm = lambda v: v.unsqueeze(2).to_broadcast([128, 16, m_])
            a1 = r1[:, :, 0:m_]; a2 = r2[:, :, 0:m_]
            tt(a1, C64[:, :, 0:m_], bcm(Ec[:, :]), ALU.mult, ["CS", "E"], ["r1"])
            tt(a2, S64[:, :, 0:m_], bcm(Es[:, :]), ALU.mult, ["CS", "E"], ["r2"])
            tt(C64[:, :, m_:2 * m_], a1, a2, ALU.subtract, ["r1", "r2"], ["CS"])
            tt(a1, S64[:, :, 0:m_], bcm(Ec[:, :]), ALU.mult, ["CS", "E"], ["r1"])
            tt(a2, C64[:, :, 0:m_], bcm(Es[:, :]), ALU.mult, ["CS", "E"], ["r2"])
            tt(S64[:, :, m_:2 * m_], a1, a2, ALU.add, ["r1", "r2"], ["CS"])
            if 2 * m_ < 64:
                tt(tA[:, :], Ec[:, :], Ec[:, :], ALU.mult, ["E"], S5K)
                tt(tB[:, :], Es[:, :], Es[:, :], ALU.mult, ["E"], S5K)
                tt(tC[:, :], Ec[:, :], Es[:, :], ALU.mult, ["E"], S5K)
                tt(Ec[:, :], tA[:, :], tB[:, :], ALU.subtract, S5K, ["E"])
                ts1(Es[:, :], tC[:, :], 2.0, ALU.mult, S5K, ["E"])
            m_ *= 2

        def bc(v16):
            return v16.unsqueeze(2).to_broadcast([128, 16, 32])

        def cmul(dr, di, ar, ai, br, bi, rk, wk, neg_im=False):
            wr = wk if isinstance(wk, tuple) else (wk, wk)
            tt(u1[:, :, :], br, bc(ar), ALU.mult, rk, ["u1"])
            tt(u2[:, :, :], bi, bc(ai), ALU.mult, rk, ["u2"])
            tt(u3[:, :, :], bi, bc(ar), ALU.mult, rk, ["u3"])
            tt(u4[:, :, :], br, bc(ai), ALU.mult, rk, ["u4"])
            tt(dr, u1[:, :, :], u2[:, :, :], ALU.subtract, ["u1", "u2"], wr[0])
            if neg_im:
                stt(di, u3[:, :, :], -1.0, u4[:, :, :], ALU.mult, ALU.subtract, ["u3", "u4"], wr[1])
            else:
                tt(di, u3[:, :, :], u4[:, :, :], ALU.add, ["u3", "u4"], wr[1])

        cmul(BBr[:, :, :], BBi[:, :, :], CRE[:, :], CIM[:, :], Bre[:, :, :], Bim[:, :, :], S5K + ["Bre", "Bim"], ["BB"])
        memset(KL[:, :, :, :], 0.0, ["KL"])
        def pc_A(k):
            cmul(Xb[k % 2][0][:, :, :], Xb[k % 2][1][:, :, :], PWr[:, k, :], PWi[:, k, :], BBr[:, :, :], BBi[:, :, :],
                 S5K + ["BB"], ["X%d" % (k % 2)])

        def pc_B(k):
            Xr_, Xi_ = Xb[k % 2]
            xk = "X%d" % (k % 2)
            kb = 6 + k % 2
            for q in range(4):
                for pl in range(4):
                    Pp = 4 * q + pl
                    o = bank(kb)[32 * pl:32 * pl + 32, 128 * q + 32 * pl:128 * q + 32 * pl + 32]
                    mm(o, Xr_[:, Pp, :], Cbr[:, Pp, :], True, False, [xk, "Cbr"], ["b%d" % kb], tile_position=(0, 32 * pl))
                    mm(o, Xi_[:, Pp, :], Cbi[:, Pp, :], False, True, [xk, "Cbi"], ["b%d" % kb], tile_position=(0, 32 * pl))
            for ri, X_ in enumerate((Xr_, Xi_)):
                bk = 2 * (k % 2) + ri
                for q in range(4):
                    tr(bankb(bk)[:, 128 * q:128 * q + 128], X_[:, 4 * q:4 * q + 4, :].rearrange("p a b -> p (a b)"), identb[:, :],
                       [xk, "identb"], ["b%d" % bk])

        def pc_C(k):
            kb = 6 + k % 2
            for b in range(4):
                cp(KL[32 * b:32 * b + 32, :, k, 32 * b:32 * b + 32],
                   bank(kb)[32 * b:32 * b + 32, :].rearrange("p (q l c) -> p q l c", q=4, l=4)[:, :, b, :], ["b%d" % kb], ["KL"],
                   eng="act")
            for ri in range(2):
                bk = 2 * (k % 2) + ri
                cp(WXS[:, :, 7 - k, ri, :], bankb(bk)[:, 0:512].rearrange("p (q m) -> p q m", q=4), ["b%d" % bk], ["WXS"], eng="act")

        def pc_D(k):
            cmul(WY[:, :, k, 0, :], WY[:, :, k, 1, :], PWr[:, k + 1, :], PWi[:, k + 1, :], Cre[:, :, :], Cim[:, :, :],
                 S5K + ["Cre", "Cim"], ["WY"], neg_im=True)

        pc_A(0)
        for k in range(8):
            pc_B(k)
            if k + 1 < 8:
                pc_A(k + 1)
            pc_D(k)
            pc_C(k)
        P.barrier()
        for nm_, ap_ in (("AR", AR), ("AI", AI), ("CRE", CRE), ("CIM", CIM), ("PWr", PWr), ("PWi", PWi), ("BBr", BBr), ("BBi", BBi),
                         ("Cre", Cre), ("Cim", Cim), ("Bre", Bre), ("LR", LR), ("LI", LI), ("DT", DT), ("T1", T1), ("T2", T2)):
            dump(nm_, ap_[:])
        dump("KL", KL[:], BF16); dump("WXS", WXS[:], BF16); dump("WY", WY[:], BF16)
        chk(3.5)
        bp = Bump(base_rt)
        XS = bp.alloc([2, 16, 64], F32)
        RR = bp.alloc([2, 16, 64], F32)
        CA = bp.alloc([3, 16], F32)
        r1 = bp.alloc([16, 64], F32); r2 = bp.alloc([16, 64], F32)
        Hb = bp.alloc([2, 16, 64], BF16)
        H0 = bp.alloc([2, 16, 16], F32)
        fin = r2[:, 0:8, :].rearrange("p a b -> p (a b)").rearrange("p (r a b) -> p r a b", r=2, a=16)
        h0st = RR[:, :, :, :].rearrange("p a b c -> p (a b c)")
        m1 = bp.alloc([2, 16], F32); m2_ = bp.alloc([2, 16], F32)
        yt = [bp.alloc([512], F32) for _ in range(2)]
        inn = [bp.alloc([512], F32) for _ in range(2)]
        sgg = [bp.alloc([512], F32) for _ in range(2)]
        gf = bp.alloc([4, 512], F32)
        gb = bp.alloc([4, 512], BF16)
        so = bp.alloc([4, 512], BF16)
        memset(CA[:, :, :], 0.0, ["CA"])
        def load_h0():
            for ri, src_ in enumerate((st_re, st_im)):
                ld(h0st[0:16, :], src_, ["h0st"])
                for Pp in range(16):
                    tr(bank(7)[:, 256 * ri + 16 * Pp:256 * ri + 16 * Pp + 16], h0st[0:16, 128 * Pp:128 * Pp + 128],
                       ident[0:16, 0:16], ["h0st", "ident"], ["b7"])
            cp(H0[:, :, :, :], bank(7).rearrange("p (r P j) -> p r P j", r=2, P=16), ["b7"], ["H0"])
            memset(RR[:, :, :, :], 0.0, ["RR0", "RR1", "h0st"])

        def s5_xs(t):
                c0, N = TILES[t]
                J = N // 8
                uk = "ub%d" % t
                for q in range(4):
                    for ri in range(2):
                        for tau in range(8):
                            for b in range(4):
                                o = bank(b).rearrange("p (r q j) -> p r q j", r=2, q=4)[:, ri, q, 0:J]
                                mm(o, WXS[32 * b:32 * b + 32, q, tau, ri, :], ub[32 * b:32 * b + 32, q, c0 + tau:c0 + N:8],
                                   tau == 0, tau == 7, ["WXS", uk], ["b%d" % b], tile_position=(32 * b, 0))

        def s5_scan(t):
                c0, N = TILES[t]
                J = N // 8
                uk = "ub%d" % t
                if t < 4:
                    for b in range(4):
                        cp(RR[:, :, b:16:4, :], bank(b).rearrange("p (r q j) -> p r q j", r=2, q=4),
                           ["b%d" % b], ["RR0", "RR1"], eng="act")
                    XSKw = ["XSb%d" % b for b in range(4)]
                    tt(r1[:, :, :], RR[:, 0, :, :], C64[:, :, :], ALU.mult, ["RR0", "CS"], ["r1", "r1b"])
                    tt(r2[:, :, :], RR[:, 1, :, :], S64[:, :, :], ALU.mult, ["RR1", "CS"], ["r2", "r2b"])
                    tt(XS[:, 0, :, :], r1[:, :, :], r2[:, :, :], ALU.add, ["r1", "r2", "r1b", "r2b"], XSKw + ["S0"])
                    tt(r1[:, :, :], RR[:, 1, :, :], C64[:, :, :], ALU.mult, ["RR1", "CS"], ["r1", "r1b"])
                    tt(r2[:, :, :], RR[:, 0, :, :], S64[:, :, :], ALU.mult, ["RR0", "CS"], ["r2", "r2b"])
                    tt(XS[:, 1, :, :], r1[:, :, :], r2[:, :, :], ALU.subtract, ["r1", "r2", "r1b", "r2b"], XSKw + ["S1"])
                    XSK = ["XSb%d" % b for b in range(4)]
                    if t > 0:
                        tt(m1[:, :, :], T2[:, :, :], CA[:, 1:3, :], ALU.mult, ["T12", "CA"], ["m1"])
                        tt(m2_[:, :, :], T1[:, :, :], CA[:, 0:2, :], ALU.mult, ["T12", "CA"], ["m2"])
                        tt(m1[:, :, :], m1[:, :, :], m2_[:, :, :], ALU.add, ["m1", "m2"], ["m1"])
                        tt(XS[:, :, :, 0], XS[:, :, :, 0], m1[:, :, :], ALU.add, XSK + ["m1"], XSK)
                    for ri in range(2):
                        for Pp in range(16):
                            P.op("dve", lambda e, ri=ri, Pp=Pp: e.tensor_tensor_scan(
                                out=RR[:, ri, Pp, :], data0=DEC[:, Pp:Pp + 1].to_broadcast([128, 64]), data1=XS[:, ri, Pp, :],
                                initial=0.0, op0=ALU.mult, op1=ALU.add), ["DEC", "XSb%d" % (Pp % 4)], ["RR%d" % ri])
                    cp(Hb[:, :, :, 0], CA[:, 0:2, :], ["CA"], ["Hb"], eng="act")
                    tt(r1[:, :, :], RR[:, 0, :, :], C64[:, :, :], ALU.mult, ["RR0", "CS"], ["r1", "r1b"])
                    tt(r2[:, :, :], RR[:, 1, :, :], S64[:, :, :], ALU.mult, ["RR1", "CS"], ["r2", "r2b"])
                    tt(XS[:, 0, :, :], r1[:, :, :], r2[:, :, :], ALU.subtract, ["r1", "r2", "r1b", "r2b"] + XSK, ["S0"])
                    tt(r1[:, :, :], RR[:, 1, :, :], C64[:, :, :], ALU.mult, ["RR1", "CS"], ["r1", "r1b"])
                    tt(r2[:, :, :], RR[:, 0, :, :], S64[:, :, :], ALU.mult, ["RR0", "CS"], ["r2", "r2b"])
                    tt(XS[:, 1, :, :], r1[:, :, :], r2[:, :, :], ALU.add, ["r1", "r2", "r1b", "r2b"] + XSK, ["S1"] + XSK)
                    cp(Hb[:, 0, :, 1:64], XS[:, 0, :, 0:63], ["S0"], ["Hb"], eng="act")
                    cp(Hb[:, 1, :, 1:64], XS[:, 1, :, 0:63], ["S1"] + XSK, ["Hb"], eng="act")
                    cp(CA[:, 0:2, :], XS[:, :, :, 63], ["S0", "S1"] + XSK, ["CA"])
                    cp(CA[:, 2, :], XS[:, 0, :, 63], ["S0"], ["CA"])
                    if t == 3:
                        for ri in range(2):
                            tr(bank(0)[0:16, 128 * ri:128 * ri + 128], CA[:, ri, :], ident[:, :], ["CA", "ident"], ["b0"])
                        cp(fin[0:16, 0, :, :].rearrange("p a b -> p (a b)"), bank(0)[0:16, 0:256], ["b0"], ["fin", "r2", "r2b"])
                        stor(nre_p, fin[0:16, 0, 0:8, :].rearrange("p a b -> p (a b)"), ["fin"], key="finp")
                        stor(nim_p, fin[0:16, 0, 8:16, :].rearrange("p a b -> p (a b)"), ["fin"], key="finp")
                else:
                    for b in range(4):
                        cp(XS[:, :, b:16:4, 0:J], bank(b).rearrange("p (r q j) -> p r q j", r=2, q=4)[:, :, :, 0:J],
                           ["b%d" % b], ["XS", "S0", "S1", "XSb0", "XSb1", "XSb2", "XSb3"], eng="act")
                    cp(Hb[:, :, :, 0:16], H0[:, :, :, :], ["H0"], ["Hb"], eng="act")
                if t == 0:
                    dump("XS", XS[:]); dump("Hb", Hb[:], BF16)

        def s5_y(t):
                c0, N = TILES[t]
                J = N // 8
                uk = "ub%d" % t
                for q in range(4):
                    for k in range(8):
                        mm(bank(4 + q)[:, 0:N].rearrange("p (j t) -> p j t", t=8)[:, :, k:8], KL[:, q, k, :],
                           ub[:, q, c0:c0 + N].rearrange("p (j t) -> p j t", t=8)[:, :, 0:8 - k], k == 0, False,
                           ["KL", uk], ["b%d" % (4 + q)])
                for q in range(4):
                    for tau in range(8):
                        for ri in range(2):
                            for b in range(4):
                                Pp = 4 * q + b
                                mm(bank(4 + q)[32 * b:32 * b + 32, tau:N:8], WY[:, Pp, tau, ri, :], Hb[:, ri, Pp, 0:J], False,
                                   (tau == 7 and ri == 1), ["WY", "Hb"], ["b%d" % (4 + q)], tile_position=(0, 32 * b))

        def s5_epi(t):
                c0, N = TILES[t]
                J = N // 8
                uk = "ub%d" % t
                for q in range(4):
                    a = q % 2
                    stt(yt[a][:, 0:N], ub[:, q, c0:c0 + N], gcol(6, q), bank(4 + q)[:, 0:N], ALU.mult, ALU.add,
                        [uk, "PVT", "b%d" % (4 + q)], ["yt%d" % a])
                    act(inn[a][:, 0:N], yt[a][:, 0:N], AF.Square, ["yt%d" % a], ["inn%d" % a], scale=math.sqrt(0.044715))
                    stt(inn[a][:, 0:N], inn[a][:, 0:N], 1.0, yt[a][:, 0:N], ALU.add, ALU.mult, ["inn%d" % a, "yt%d" % a], ["inn%d" % a])
                    act(sgg[a][:, 0:N], inn[a][:, 0:N], AF.Sigmoid, ["inn%d" % a], ["sgg%d" % a], scale=2.0 * math.sqrt(2.0 / math.pi))
                    tt(gf[:, q, 0:N], yt[a][:, 0:N], sgg[a][:, 0:N], ALU.mult, ["yt%d" % a, "sgg%d" % a], ["gf"])
                    cp(gb[:, q, 0:N], gf[:, q, 0:N], ["gf"], ["gb"], eng="act")
                for oc in range(4):
                    a = 4 + oc % 2
                    for kc in range(4):
                        mm(bank(a)[:, 0:N], wglu[:, kc, 128 * oc:128 * oc + 128], gb[:, kc, 0:N], kc == 0, kc == 3,
                           ["wglu", "gb"], ["b%d" % a])
                    act(sgg[oc % 2][:, 0:N], bank(a)[:, 0:N], AF.Sigmoid, ["b%d" % a], ["sgg%d" % (oc % 2)])
                    tt(so[:, oc, 0:N], gf[:, oc, 0:N], sgg[oc % 2][:, 0:N], ALU.mult, ["gf", "sgg%d" % (oc % 2)], ["so"])
                for oc in range(8):
                    a = 6 + oc % 2
                    for kc in range(4):
                        mm(bank(a)[:, 0:N], woh[:, kc, 128 * oc:128 * oc + 128], so[:, kc, 0:N], kc == 0, kc == 3,
                           ["woh", "so"], ["b%d" % a])
                    tt(xT[:, oc, c0:c0 + N], bank(a)[:, 0:N], xT[:, oc, c0:c0 + N], ALU.add, ["b%d" % a, "x%d" % t], ["x%d" % t])
                if t == 4:
                    a8r = T1[:, 0, :].unsqueeze(2).to_broadcast([128, 16, 16])
                    a8i = T2[:, 1, :].unsqueeze(2).to_broadcast([128, 16, 16])
                    w1 = yt[0][:, 0:256].rearrange("p (a b) -> p a b", b=16)
                    w2 = yt[1][:, 0:256].rearrange("p (a b) -> p a b", b=16)
                    tt(w1, H0[:, 0, :, :], a8r, ALU.mult, ["H0", "T12"], ["yt0"])
                    tt(w2, H0[:, 1, :, :], a8i, ALU.mult, ["H0", "T12"], ["yt1"])
                    tt(w1, w1, w2, ALU.subtract, ["yt0", "yt1"], ["yt0"])
                    tt(fin[:, 0, :, :], w1, XS[:, 0, :, 0:16], ALU.add, ["yt0", "XS"], ["fin", "r2", "r2b"])
                    tt(w1, H0[:, 1, :, :], a8r, ALU.mult, ["H0", "T12"], ["yt0"])
                    tt(w2, H0[:, 0, :, :], a8i, ALU.mult, ["H0", "T12"], ["yt1"])
                    tt(w1, w1, w2, ALU.add, ["yt0", "yt1"], ["yt0"])
                    tt(fin[:, 1, :, :], w1, XS[:, 1, :, 0:16], ALU.add, ["yt0", "XS"], ["fin", "r2", "r2b"])
                    for ri, dst in enumerate((nre_s, nim_s)):
                        for Pp in range(16):
                            bk = Pp // 4
                            tr(bank(bk)[0:16, 128 * (Pp % 4):128 * (Pp % 4) + 128], fin[:, ri, Pp, :], ident[:, :],
                               ["fin", "ident"], ["b%d" % bk])
                        for bk in range(4):
                            cp(h0st[0:16, 512 * bk:512 * bk + 512], bank(bk)[0:16, :], ["b%d" % bk], ["h0st", "RR0", "RR1"])
                        stor(dst, h0st[0:16, :], ["h0st"], key="h0st_out")
        s5_xs(0)
        load_h0()
        s5_scan(0)
        s5_xs(1)
        for t in range(5):
            s5_y(t)
            if t + 1 < 5:
                s5_scan(t + 1)
            if t + 2 < 5:
                s5_xs(t + 2)
            s5_epi(t)
        P.barrier()

        if "x2" in dbg:
            stor(dbg_out["x2"], xT[:, :, :], ["x0", "x1", "x2", "x3", "x4"], key="dbgx2")
            P.barrier()
        chk(4)
        bp = Bump()
        wq = bp.alloc([8, 1024], BF16)
        wo = bp.alloc([8, 1024], BF16)
        KT = bp.alloc([8, 256], BF16)
        Vb = bp.alloc([2, 1024], BF16)
        mx = bp.alloc([4], F32); nb = bp.alloc([4], F32); ssum = bp.alloc([4], F32); rsm = bp.alloc([4], F32)
        base_a = bp.off
        xnt = bp.alloc([8, 512], BF16)
        sq = bp.alloc([8, 512], BF16)
        rstd = bp.alloc([512], F32)
        qT = [bp.alloc([8, 512], BF16) for _ in range(2)]
        qTs = bp.alloc([8, 128], BF16)
        base_b = bp.off
        wk = bp.alloc([8, 1024], BF16)
        wv = bp.alloc([8, 1024], BF16)
        mst = bp.alloc([1024], F32)
        mrb = bp.alloc([1024], BF16)
        mnT = bp.alloc([8, 256], BF16)
        kvo = [bp.alloc([1024], F32) for _ in range(2)]
        junk = bp.alloc([1024], BF16)
        ld(wq[:, :, :], wview(w_q), ["wq"], eng="pool")
        ld(wk[:, :, :], wview(w_mem_k), ["wk"], eng="pool")
        ld(wv[:, :, :], wview(w_mem_v), ["wv"], eng="pool")
        ld(wo[:, :, :], wview(w_o), ["wo"], eng="pool")
        bp = Bump(base_b)
        Pf = bp.alloc([4, 256], F32)
        Pb = bp.alloc([4, 256], BF16)
        PT = bp.alloc([2, 4, 512], BF16)
        oT = bp.alloc([8, 512], BF16)
        oTs = bp.alloc([8, 128], BF16)
        Ks = [bp.alloc([2, 1024], BF16) for _ in range(2)]
        NV = 4
        Vs = [bp.alloc([2, 1024], BF16) for _ in range(NV)]
        KTs = [bp.alloc([8, 256], BF16) for _ in range(2)]
        PTs = bp.alloc([2, 4, 128], BF16)
        m2x = bp.alloc([2], F32)
        SCALE = 1.0 / 16.0
        SETS = ((2, 3), (5, 7))

        def softmax_rows(st_):
            pa, pb_ = SETS[st_]
            for hh, bk in enumerate((pa, pb_)):
                P.op("dve", lambda e, hh=hh, bk=bk: e.tensor_reduce(
                    out=m2x[:, hh:hh + 1], in_=bank(bk), axis=mybir.AxisListType.X, op=ALU.max, negate=True),
                    ["b%d" % bk], ["m2x"])
            tt(nb[:, 0:1], m2x[:, 0:1], m2x[:, 1:2], ALU.min, ["m2x"], ["nb"])
            for h in range(4):
                bk = (pa, pb_)[h // 2]
                P.op("act", lambda e, h=h, bk=bk: e.activation(
                    out=Pf[:, h, :], in_=bank(bk)[:, 256 * (h % 2):256 * (h % 2) + 256], func=AF.Exp,
                    bias=nb[:, 0:1], scale=1.0, accum_out=ssum[:, h:h + 1]), ["b%d" % bk, "nb"], ["Pf%d" % (h // 2), "ssum%d" % h])
            recip(rsm[:, :], ssum[:, :], ["ssum0", "ssum1", "ssum2", "ssum3"], ["rsm"])
            for h in range(4):
                act(Pb[:, h, :], Pf[:, h, :], AF.Copy, ["Pf%d" % (h // 2), "rsm"], ["Pb"], scale=rsm[:, h:h + 1])

        qcnt = [0]

        def q_group(t, hd):
            c0, N = TILES[t]
            a = qcnt[0] % 2
            qcnt[0] += 1
            dstq = qT[t % 2][:, hd, 0:N] if t < 4 else qTs[:, hd, 0:N]
            qk = ("qT%d" % (t % 2)) if t < 4 else "qTs"
            for kc in range(8):
                mm(bank(a)[:, 0:N], wq[:, kc, 128 * hd:128 * hd + 128], xnt[:, kc, 0:N], kc == 0, kc == 7, ["wq", "xnt"], ["b%d" % a])
            act(dstq, bank(a)[:, 0:N], AF.Copy, ["b%d" % a], [qk], scale=SCALE)

        def o_group(t, oc):
            c0, N = TILES[t]
            a = qcnt[0] % 2
            qcnt[0] += 1
            src_ = oT if t < 4 else oTs
            sk = "oT" if t < 4 else "oTs"
            for kc in range(8):
                mm(bank(a)[:, 0:N], wo[:, kc, 128 * oc:128 * oc + 128], src_[:, kc, 0:N], kc == 0, kc == 7, ["wo", sk], ["b%d" % a])
            tt(xT[:, oc, c0:c0 + N], bank(a)[:, 0:N], xT[:, oc, c0:c0 + N], ALU.add, ["b%d" % a, "x%d" % t], ["x%d" % t])

        def n_stage(t):
            norm_tile(t, 2, lambda kc, N: xnt[:, kc, 0:N], "xnt", sq, rstd, 6)

        def s_stage(t, tc, st_):
            for h in range(4):
                pb_ = SETS[st_][h // 2]
                for dc in range(2):
                    mm(bank(pb_)[:, 256 * (h % 2):256 * (h % 2) + 256], qT[t % 2][:, 2 * h + dc, 128 * tc:128 * tc + 128], KT[:, 2 * h + dc, :],
                       dc == 0, dc == 1, ["qT%d" % (t % 2), "KT"], ["b%d" % pb_])

        def t_stage(tc):
            for mc in range(2):
                for h in range(4):
                    tr(bankb(4)[:, 128 * (4 * mc + h):128 * (4 * mc + h) + 128], Pb[:, h, 128 * mc:128 * mc + 128], identb[:, :],
                       ["Pb", "identb"], ["b4"])
            evac(PT[:, :, :, 128 * tc:128 * tc + 128], bankb(4).rearrange("p (m h t) -> p m h t", m=2, h=4), ["b4"], ["PT"])

        def pv_stage():
            for hd in range(8):
                a = qcnt[0] % 2
                qcnt[0] += 1
                for mc in range(2):
                    mm(bank(a)[:, :], Vb[:, mc, 128 * hd:128 * hd + 128], PT[:, mc, hd // 2, :], mc == 0, mc == 1, ["Vb", "PT"], ["b%d" % a])
                evac(oT[:, hd, :], bank(a)[:, :], ["b%d" % a], ["oT"])

        def load_k(s):
            sl = s % 2
            ld(Ks[sl][:, :, :], ck[s].rearrange("(mc p) f -> p mc f", p=128), ["Ks%d" % sl], eng="pool")

        def load_v(s):
            sv = s % NV
            ld(Vs[sv][:, :, :], cv[s].rearrange("(mc p) f -> p mc f", p=128), ["Vs%d" % sv], eng="pool")

        def sg_A(G, si):
            s = 4 * G + si
            sl = s % 2
            for mc in range(2):
                bk = 6 if mc == 0 else 4
                for hd in range(8):
                    tr(bankb(bk)[:, 128 * hd:128 * hd + 128], Ks[sl][:, mc, 128 * hd:128 * hd + 128], identb[:, :],
                       ["Ks%d" % sl, "identb"], ["b%d" % bk])
                evac(KTs[sl][:, :, 128 * mc:128 * mc + 128], bankb(bk).rearrange("p (h m) -> p h m", h=8),
                     ["b%d" % bk], ["KTs%d" % sl])
            for h in range(4):
                pb_ = SETS[1][h // 2]
                for dc in range(2):
                    mm(bank(pb_)[32 * si:32 * si + 8, 256 * (h % 2):256 * (h % 2) + 256], qTs[:, 2 * h + dc, 8 * s:8 * s + 8],
                       KTs[sl][:, 2 * h + dc, :], dc == 0, dc == 1, ["qTs", "KTs%d" % sl], ["b%d" % pb_], tile_position=(0, 32 * si))
            if s + 2 < 16:
                load_k(s + 2)

        def sg_C(G):
            for mc in range(2):
                for h in range(4):
                    tr(bankb(4)[:, 128 * (4 * mc + h):128 * (4 * mc + h) + 128], Pb[:, h, 128 * mc:128 * mc + 128], identb[:, :],
                       ["Pb", "identb"], ["b4"])
            evac(PTs[:, :, :, :], bankb(4).rearrange("p (m h t) -> p m h t", m=2, h=4), ["b4"], ["PTs"])
            for si in range(4):
                s = 4 * G + si
                sv = s % NV
                for hd in range(8):
                    for mc in range(2):
                        mm(bank(hd // 4)[:, 32 * (hd % 4) + 8 * si:32 * (hd % 4) + 8 * si + 8], Vs[sv][:, mc, 128 * hd:128 * hd + 128],
                           PTs[:, mc, hd // 2, 32 * si:32 * si + 8], mc == 0, mc == 1, ["Vs%d" % sv, "PTs"], ["b%d" % (hd // 4)])
                if s + NV < 16:
                    load_v(s + NV)
            for hh in range(2):
                evac(oTs[:, 4 * hh:4 * hh + 4, 32 * G:32 * G + 32], bank(hh)[:, 0:128].rearrange("p (h t) -> p h t", h=4),
                     ["b%d" % hh], ["oTs"])

        n_stage(0)
        for hd in range(8):
            q_group(0, hd)
        n_stage(4)
        for hd in range(8):
            q_group(4, hd)
        n_stage(1)
        chk(4.1)
        for mc in range(2):
            ld(mst[:, :], memp[128 * mc:128 * mc + 128, :], ["mst"])
            P.op("act", lambda e: e.activation(out=junk[:, :], in_=mst[:, :], func=AF.Square, accum_out=ssum[:, 0:1]),
                 ["mst"], ["junk", "ssum"])
            act(rsm[:, 0:1], ssum[:, 0:1], AF.Sqrt, ["ssum"], ["rsm"], bias=EPS, scale=1.0 / 1024.0)
            recip(rsm[:, 0:1], rsm[:, 0:1], ["rsm"], ["rsm"])
            ts1(mrb[:, :], mst[:, :], rsm[:, 0:1], ALU.mult, ["mst", "rsm"], ["mrb"])
            for kc in range(8):
                tr(bankb(7)[:, 128 * kc:128 * kc + 128], mrb[:, 128 * kc:128 * kc + 128], identb[:, :], ["mrb", "identb"], ["b7"])
            for kc in range(8):
                ts1(mnT[:, kc, 128 * mc:128 * mc + 128], bankb(7)[:, 128 * kc:128 * kc + 128], gcol(5, kc), ALU.mult,
                    ["b7", "PVT"], ["mnT"])
        chk(4.2)
        cntk = [0]
        for (wsrc, wkey, dst, isv) in ((wk, "wk", nk_p, False), (wv, "wv", nv_p, True)):
            for mc in range(2):
                s = cntk[0] % 2
                cntk[0] += 1
                for hh in range(2):
                    a = hh
                    for kc in range(8):
                        mm(bank(a)[:, :], mnT[:, kc, 128 * mc:128 * mc + 128], wsrc[:, kc, 512 * hh:512 * hh + 512],
                           kc == 0, kc == 7, ["mnT", wkey], ["b%d" % a])
                    cp(kvo[s][:, 512 * hh:512 * hh + 512], bank(a)[:, :], ["b%d" % a], ["kvo%d" % s], eng="act")
                    if isv:
                        cp(Vb[:, mc, 512 * hh:512 * hh + 512], kvo[s][:, 512 * hh:512 * hh + 512], ["kvo%d" % s], ["Vb"])
                stor(dst[128 * mc:128 * mc + 128, :], kvo[s][:, :], ["kvo%d" % s])
        chk(4.3)
        for hd in range(8):
            a = 2 + hd % 2
            for kc in range(8):
                mm(bank(a)[:, 0:256], wk[:, kc, 128 * hd:128 * hd + 128], mnT[:, kc, :], kc == 0, kc == 7, ["wk", "mnT"], ["b%d" % a])
            evac(KT[:, hd, :], bank(a)[:, 0:256], ["b%d" % a], ["KT"])
        P.barrier()
        chk(4.5)
        load_k(0); load_k(1)
        for s_ in range(NV):
            load_v(s_)
        for t in range(4):
            for tc in range(4):
                s_stage(t, tc, 0)
                softmax_rows(0)
                if t + 1 < 4:
                    q_group(t + 1, 2 * tc)
                    q_group(t + 1, 2 * tc + 1)
                if t >= 1:
                    o_group(t - 1, 2 * tc)
                    o_group(t - 1, 2 * tc + 1)
                sg_A(t, tc)
                t_stage(tc)
            softmax_rows(1)
            pv_stage()
            if t + 2 < 4:
                n_stage(t + 2)
            sg_C(t)
        for oc in range(8):
            o_group(3, oc)
        for oc in range(8):
            o_group(4, oc)
        P.barrier()

        if "x3" in dbg:
            stor(dbg_out["x3"], xT[:, :, :], ["x0", "x1", "x2", "x3", "x4"], key="dbgx3")
            P.barrier()
        chk(5)

        fin_bufs = {}

        def fin_alloc(bp_, sq_, rstd_):
            fin_bufs["yT"] = bp_.alloc([8, 512], F32)
            fin_bufs["oy"] = [bp_.alloc([1024], F32) for _ in range(2)]
            fin_bufs["sq"] = sq_
            fin_bufs["rstd"] = rstd_

        oc_ = [0]

        def fin_A(t):
            c0, N = TILES[t]
            act(fin_bufs["sq"][:, :, 0:N], xT[:, :, c0:c0 + N], AF.Square, ["x%d" % t], ["sq"])

        def fin_B(t):
            c0, N = TILES[t]
            sq_, rstd_, ys = fin_bufs["sq"], fin_bufs["rstd"], fin_bufs["yT"]
            for kc in range(8):
                mm(bank(6)[:, 0:N], onesb[:, :], sq_[:, kc, 0:N], kc == 0, kc == 7, ["onesb", "sq"], ["b6"])
            act(rstd_[:, 0:N], bank(6)[:, 0:N], AF.Sqrt, ["b6"], ["rstd"], bias=EPS, scale=1.0)
            recip(rstd_[:, 0:N], rstd_[:, 0:N], ["rstd"], ["rstd"])
            for kc in range(8):
                stt(ys[:, kc, 0:N], xT[:, kc, c0:c0 + N], gcol(4, kc), rstd_[:, 0:N], ALU.mult, ALU.mult,
                    ["x%d" % t, "PVT", "rstd"], ["yTf"])

        def fin_C(t):
            c0, N = TILES[t]
            ys = fin_bufs["yT"]
            for ch in range(N // 128):
                s_ = oc_[0] % 2
                oc_[0] += 1
                for h in range(2):
                    for k4 in range(4):
                        kc = 4 * h + k4
                        tr(bank(7)[:, 128 * k4:128 * k4 + 128], ys[:, kc, 128 * ch:128 * ch + 128], ident[:, :],
                           ["yTf", "ident"], ["b7"])
                    evac(fin_bufs["oy"][s_][:, 512 * h:512 * h + 512], bank(7)[:, :], ["b7"], ["oy%d" % s_])
                dst = y_p[c0 + 128 * ch:c0 + 128 * ch + 128, :] if t < 4 else y_s
                stor(dst, fin_bufs["oy"][s_][:, :], ["oy%d" % s_])

        lagq = []

        def final_tile(t):
            if t is None:
                fin_C(lagq[-2]); fin_B(lagq[-1]); fin_C(lagq[-1])
                return
            lagq.append(t)
            i = len(lagq) - 1
            if i >= 2:
                fin_C(lagq[i - 2])
            if i >= 1:
                fin_B(lagq[i - 1])
            fin_A(t)

        ffn(3, w_gate2, w_up2, w_down2, "f2", tail_cb=final_tile, extra=fin_alloc)
        P.emit(st)
    return nc, P


_CACHE = {}


def _prep_inputs(inp):
    f = lambda a: np.ascontiguousarray(np.asarray(a, dtype=np.float32))
    vecs = np.zeros((64, 1024), np.float32)
    vecs[0] = inp["g_ffn1"][0]; vecs[1] = inp["g_mix"][0]; vecs[2] = inp["g_xattn"][0]; vecs[3] = inp["g_ffn2"][0]
    vecs[4] = inp["g_final"]; vecs[5] = inp["g_mem"][0]
    vecs[6, 0:512] = inp["ssm_d"][0]; vecs[6, 512:] = inp["conv_b"][0]
    vecs[7, 0:512] = inp["conv_ln_g"][0]; vecs[7, 512:] = inp["conv_ln_b"][0]
    vecs[8:39, 0:512] = inp["conv_w"][0]
    shared = dict(
        vecs=vecs, w_mem_k=f(inp["w_mem_k"][0]), w_mem_v=f(inp["w_mem_v"][0]),
        w_gate1=f(inp["w_ffn1_gate"][0]), w_up1=f(inp["w_ffn1_up"][0]), w_down1=f(inp["w_ffn1_down"][0]),
        w_in=f(inp["w_in"][0]), a_re=f(inp["ssm_a_re"][0]), a_im=f(inp["ssm_a_im"][0]), log_dt=f(inp["ssm_log_dt"]),
        b_re=f(inp["ssm_b_re"][0]), b_im=f(inp["ssm_b_im"][0]), c_re=f(inp["ssm_c_re"][0]), c_im=f(inp["ssm_c_im"][0]),
        w_glu=f(inp["w_ssm_glu"][0]), w_out=f(inp["w_out"][0]), w_q=f(inp["w_mem_q"][0]), w_o=f(inp["w_mem_o"][0]),
        w_gate2=f(inp["w_ffn2_gate"][0]), w_up2=f(inp["w_ffn2_up"][0]), w_down2=f(inp["w_ffn2_down"][0]),
        ident=np.eye(128, dtype=np.float32))
    maps = []
    for c in range(8):
        sl = slice(16 * c, 16 * c + 16)
        m = dict(shared)
        m["xp"] = f(inp["x_prompt"][c])
        m["xs"] = f(inp["x_sample"][sl]).reshape(128, 1024)
        m["st_re"] = f(inp["state_ssm_re"][0, sl]).reshape(16, 2048)
        m["st_im"] = f(inp["state_ssm_im"][0, sl]).reshape(16, 2048)
        m["cconv"] = f(inp["cache_conv"][0, sl])
        m["ck"] = f(inp["cache_mem_k"][0, sl]).reshape(16, 256, 1024)
        m["cv"] = f(inp["cache_mem_v"][0, sl]).reshape(16, 256, 1024)
        m["memp"] = f(inp["mem_prompt"][c])
        maps.append(m)
    return maps


def kernel(**inputs):
    if "nc" not in _CACHE:
        _CACHE["nc"] = build_nc()[0]
    nc = _CACHE["nc"]
    maps = _prep_inputs(inputs)
    res = run_bass_kernel_spmd(nc, maps, core_ids=list(range(8)))
    R = res.results
    y_prompt = np.stack([R[c]["y_p"] for c in range(8)])
    y_sample = np.concatenate([R[c]["y_s"].reshape(16, 8, 1024) for c in range(8)])
    nre_p = np.stack([R[c]["nre_p"].reshape(32, 64) for c in range(8)])[None]
    nim_p = np.stack([R[c]["nim_p"].reshape(32, 64) for c in range(8)])[None]
    nconv_p = np.stack([R[c]["nconv_p"] for c in range(8)])[None]
    nk_p = np.stack([R[c]["nk_p"].reshape(256, 4, 256) for c in range(8)])[None]
    nv_p = np.stack([R[c]["nv_p"].reshape(256, 4, 256) for c in range(8)])[None]
    nre_s = np.concatenate([R[c]["nre_s"].reshape(16, 32, 64) for c in range(8)])[None]
    nim_s = np.concatenate([R[c]["nim_s"].reshape(16, 32, 64) for c in range(8)])[None]
    nconv_s = np.concatenate([R[c]["nconv_s"] for c in range(8)])[None]
    return (y_prompt.astype(np.float32), y_sample.astype(np.float32), nre_p, nim_p, nconv_p, nk_p, nv_p,
            nre_s, nim_s, nconv_s)
```

```python
import contextlib
import math
import numpy as np
import concourse.bass as bass
import concourse.mybir as mybir
from concourse.bass_utils import run_bass_kernel_spmd

F32 = mybir.dt.float32
BF16 = mybir.dt.bfloat16
I32 = mybir.dt.int32
AF = mybir.ActivationFunctionType
ALU = mybir.AluOpType
ENGS = ("pe", "act", "dve", "pool", "sp")
EPS = 1e-6
NT = 2176
TILES = [(0, 512), (512, 512), (1024, 512), (1536, 512), (2048, 128)]
AW = 34800


class Prog:
    def __init__(self, nc):
        self.nc = nc
        self.ops = []

    frozen = False

    def op(self, eng, fn, reads=(), writes=()):
        if self.frozen:
            return
        self.ops.append(dict(eng=eng, fn=fn, r=tuple(reads), w=tuple(writes), dma=False, key=None, bar=False))

    def dma(self, eng, fn, reads=(), writes=(), key=None):
        if self.frozen:
            return
        if key is None:
            key = writes[0] if writes else reads[0]
        self.ops.append(dict(eng=eng, fn=fn, r=tuple(reads), w=tuple(writes), dma=True, key=key, bar=False))

    def barrier(self):
        if self.frozen:
            return
        self.ops.append(dict(bar=True, eng=None, dma=False))

    def emit(self, stack):
        nc = self.nc
        ops = self.ops
        n = len(ops)
        last_w, rd_since = {}, {}
        deps = [set() for _ in range(n)]
        last_on_eng, last_dma_key = {}, {}
        pending = {e: set() for e in ENGS}
        for i, o in enumerate(ops):
            if o["bar"]:
                bd = set(last_on_eng.values()) | set(last_dma_key.values())
                for e in ENGS:
                    pending[e] = set(bd)
                continue
            d = deps[i]
            d |= pending[o["eng"]]
            pending[o["eng"]] = set()
            for r in o["r"]:
                if r in last_w:
                    d.add(last_w[r])
            for w in o["w"]:
                if w in last_w:
                    d.add(last_w[w])
                d.update(rd_since.get(w, ()))
            d.discard(i)
            for r in o["r"]:
                lst = rd_since.setdefault(r, [])
                if not o["dma"]:
                    lst[:] = [j for j in lst if ops[j]["dma"] or ops[j]["eng"] != o["eng"]]
                lst.append(i)
            for w in o["w"]:
                last_w[w] = i
                rd_since[w] = []
            if o["dma"]:
                last_dma_key[o["key"]] = i
            else:
                last_on_eng[o["eng"]] = i
        marked = [False] * n
        for i, o in enumerate(ops):
            if o["bar"]:
                continue
            for j in deps[i]:
                pj = ops[j]
                if pj["dma"]:
                    continue
                if pj["eng"] == o["eng"] == "pe" and not o["dma"]:
                    continue
                marked[j] = True
        eng_cnt = {e: 0 for e in ENGS}
        sigval = [0] * n
        dma_keys = []
        for i, o in enumerate(ops):
            if o["bar"]:
                continue
            if o["dma"]:
                if o["key"] not in dma_keys:
                    dma_keys.append(o["key"])
            elif marked[i]:
                eng_cnt[o["eng"]] += 1
                sigval[i] = eng_cnt[o["eng"]]
        sems = {e: stack.enter_context(nc.semaphore("s_" + e)) for e in ENGS}
        dsems = {k: stack.enter_context(nc.semaphore("d_%d" % i)) for i, k in enumerate(dma_keys)}
        waits = [None] * n
        waited = {e: {} for e in ENGS}
        run_dma = {}
        for i, o in enumerate(ops):
            if o["bar"]:
                continue
            need = {}
            for j in deps[i]:
                pj = ops[j]
                if pj["dma"]:
                    s = ("d", pj["key"])
                    v = 16 * run_dma[pj["key"]]
                else:
                    if pj["eng"] == o["eng"] == "pe" and not o["dma"]:
                        continue
                    s = ("e", pj["eng"])
                    v = sigval[j]
                if need.get(s, 0) < v:
                    need[s] = v
            wl = []
            wd = waited[o["eng"]]
            for s, v in need.items():
                if wd.get(s, 0) < v:
                    wd[s] = v
                    wl.append((s, v))
            waits[i] = wl
            if o["dma"]:
                run_dma[o["key"]] = run_dma.get(o["key"], 0) + 1
        final_dma = dict(run_dma)
        prog = {e: [i for i, o in enumerate(ops) if not o["bar"] and o["eng"] == e] for e in ENGS}
        pc = {e: 0 for e in ENGS}
        val = {}
        progress = True
        while progress:
            progress = False
            for e in ENGS:
                while pc[e] < len(prog[e]):
                    i = prog[e][pc[e]]
                    if all(val.get(s_, 0) >= v for s_, v in waits[i]):
                        o = ops[i]
                        if o["dma"]:
                            val[("d", o["key"])] = val.get(("d", o["key"]), 0) + 16
                        elif marked[i]:
                            val[("e", e)] = val.get(("e", e), 0) + 1
                        pc[e] += 1
                        progress = True
                    else:
                        break
        stuck = {e: (pc[e], len(prog[e])) for e in ENGS if pc[e] < len(prog[e])}
        if stuck:
            for e in stuck:
                i = prog[e][pc[e]]
                print("DEADLOCK", e, "op", i, "waits", waits[i], "have", {s_: val.get(s_, 0) for s_, _ in waits[i]}, ops[i]["r"], ops[i]["w"])
            raise RuntimeError("semaphore deadlock in recorded program: %s" % stuck)
        self.stats = dict(n_ops=n, n_waits=sum(len(w) for w in waits if w), n_marked=sum(marked),
                          n_dma_sems=len(dma_keys), per_eng={e: sum(1 for o in ops if o["eng"] == e) for e in ENGS})
        block = stack.enter_context(nc.Block())

        def run_engine(ename, e):
            for i, o in enumerate(ops):
                if o["eng"] != ename:
                    continue
                for (s, v) in waits[i]:
                    e.wait_ge(dsems[s[1]] if s[0] == "d" else sems[s[1]], v)
                ins = o["fn"](e)
                if o["dma"]:
                    ins.then_inc(dsems[o["key"]], 16)
                elif marked[i]:
                    ins.then_inc(sems[ename], 1)
            if ename == "sp":
                for k, c in final_dma.items():
                    e.wait_ge(dsems[k], 16 * c)
                for en in ENGS:
                    if en != "sp" and eng_cnt[en] > 0:
                        e.wait_ge(sems[en], eng_cnt[en])

        block.tensor(lambda e: run_engine("pe", e))
        block.scalar(lambda e: run_engine("act", e))
        block.vector(lambda e: run_engine("dve", e))
        block.gpsimd(lambda e: run_engine("pool", e))
        block.sync(lambda e: run_engine("sp", e))


def build_nc(dbg=(), upto=99):
    nc = bass.Bass("TRN2", target_bir_lowering=False)

    def din(name, shape):
        return nc.dram_tensor(name, list(shape), F32, kind="ExternalInput").ap()

    def dout(name, shape):
        return nc.dram_tensor(name, list(shape), F32, kind="ExternalOutput").ap()

    xp = din("xp", [2048, 1024]); xs = din("xs", [128, 1024])
    st_re = din("st_re", [16, 2048]); st_im = din("st_im", [16, 2048])
    cconv = din("cconv", [16, 30, 512])
    ck = din("ck", [16, 256, 1024]); cv = din("cv", [16, 256, 1024])
    memp = din("memp", [256, 1024])
    vecs = din("vecs", [64, 1024])
    w_mem_k = din("w_mem_k", [1024, 1024]); w_mem_v = din("w_mem_v", [1024, 1024])
    w_gate1 = din("w_gate1", [1024, 2816]); w_up1 = din("w_up1", [1024, 2816]); w_down1 = din("w_down1", [2816, 1024])
    w_in = din("w_in", [1024, 1536])
    a_re = din("a_re", [32, 64]); a_im = din("a_im", [32, 64]); log_dt = din("log_dt", [1, 32])
    b_re = din("b_re", [32, 64, 16]); b_im = din("b_im", [32, 64, 16])
    c_re = din("c_re", [32, 16, 64]); c_im = din("c_im", [32, 16, 64])
    w_glu = din("w_glu", [512, 512]); w_out = din("w_out", [1024, 1024])
    w_q = din("w_q", [1024, 1024]); w_o = din("w_o", [1024, 1024])
    w_gate2 = din("w_gate2", [1024, 2816]); w_up2 = din("w_up2", [1024, 2816]); w_down2 = din("w_down2", [2816, 1024])
    ident_d = din("ident", [128, 128])

    y_p = dout("y_p", [2048, 1024]); y_s = dout("y_s", [128, 1024])
    nre_p = dout("nre_p", [16, 128]); nim_p = dout("nim_p", [16, 128])
    nconv_p = dout("nconv_p", [30, 512])
    nk_p = dout("nk_p", [256, 1024]); nv_p = dout("nv_p", [256, 1024])
    nre_s = dout("nre_s", [16, 2048]); nim_s = dout("nim_s", [16, 2048])
    nconv_s = dout("nconv_s", [16, 30, 512])
    dbg_out = {k: dout("dbg_" + k, [128, 8, NT]) for k in dbg if k.startswith("x")}
    dumps = []

    def dump(name, ap, dt=F32):
        if "s5" not in dbg:
            return
        shp = [int(x) for x in ap.shape]
        d = nc.dram_tensor("dmp_" + name, shp, dt, kind="ExternalOutput").ap()
        P.barrier()
        P.dma("sp", lambda e: e.dma_start(out=d, in_=ap), (), (), key="dmp_" + name)
        P.barrier()

    P = Prog(nc)

    def chk(k):
        if k >= upto:
            P.frozen = True

    st = contextlib.ExitStack()
    with st:
        st.enter_context(nc.allow_non_contiguous_dma(reason="small strided parameter loads"))
        xT = st.enter_context(nc.sbuf_tensor("xT", [128, 8, NT], F32))
        arena = st.enter_context(nc.sbuf_tensor("arena", [128, AW], F32))
        ident = st.enter_context(nc.sbuf_tensor("identf", [128, 128], F32))
        identb = st.enter_context(nc.sbuf_tensor("identb", [128, 128], BF16))
        onesb = st.enter_context(nc.sbuf_tensor("onesb", [128, 128], BF16))
        ones512 = st.enter_context(nc.sbuf_tensor("ones512", [128, 128], BF16))
        PVT = st.enter_context(nc.sbuf_tensor("PVT", [128, 8, 64], F32))
        banks = [st.enter_context(nc.psum_tensor("bank%d" % i, [128, 512], F32)) for i in range(8)]

        def bank(i):
            return banks[i][:, :]

        def bankb(i):
            return banks[i][:, :].bitcast(BF16)

        class Bump:
            def __init__(self, base=0):
                self.off = base

            def alloc(self, shape, dt):
                nel = int(np.prod(shape))
                words = (nel * (4 if dt in (F32, I32) else 2) + 3) // 4
                v = arena[:, self.off:self.off + words]
                self.off += words
                assert self.off <= AW, ("arena overflow", self.off)
                if dt != F32:
                    v = v.bitcast(dt)
                    v = v[:, 0:nel]
                if len(shape) > 1:
                    names = ["d%d" % i for i in range(len(shape))]
                    pat = "p (" + " ".join(names) + ") -> p " + " ".join(names)
                    v = v.rearrange(pat, **{nm: s for nm, s in zip(names[1:], shape[1:])})
                return v

        def mm(out, lhsT, rhs, start, stop, reads, writes, **kw):
            P.op("pe", lambda e: e.matmul(out, lhsT=lhsT, rhs=rhs, start=start, stop=stop, **kw), reads, writes)

        def tr(out, in_, idn, reads, writes):
            P.op("pe", lambda e: e.transpose(out=out, in_=in_, identity=idn), reads, writes)

        def act(out, in_, func, reads, writes, **kw):
            P.op("act", lambda e: e.activation(out=out, in_=in_, func=func, **kw), reads, writes)

        def tt(out, in0, in1, op, reads, writes, eng="dve"):
            P.op(eng, lambda e: e.tensor_tensor(out=out, in0=in0, in1=in1, op=op), reads, writes)

        def ts(out, in0, s1, s2, op0, op1, reads, writes, eng="dve"):
            P.op(eng, lambda e: e.tensor_scalar(out=out, in0=in0, scalar1=s1, scalar2=s2, op0=op0, op1=op1), reads, writes)

        def ts1(out, in0, s1, op0, reads, writes, eng="dve"):
            P.op(eng, lambda e: e.tensor_scalar(out=out, in0=in0, scalar1=s1, scalar2=None, op0=op0), reads, writes)

        def stt(out, in0, scalar, in1, op0, op1, reads, writes):
            P.op("dve", lambda e: e.scalar_tensor_tensor(out=out, in0=in0, scalar=scalar, in1=in1, op0=op0, op1=op1), reads, writes)

        def cp(out, in_, reads, writes, eng="dve"):
            if eng == "act":
                P.op("act", lambda e: e.activation(out=out, in_=in_, func=AF.Copy), reads, writes)
            else:
                P.op(eng, lambda e: e.tensor_copy(out=out, in_=in_), reads, writes)

        def memset(ap, val, writes, eng="dve"):
            P.op(eng, lambda e: e.memset(ap, val), (), writes)

        def recip(out, in_, reads, writes):
            P.op("dve", lambda e: e.reciprocal(out=out, in_=in_), reads, writes)

        def ld(out, in_, writes, eng="sp", key=None):
            P.dma(eng, lambda e: e.dma_start(out=out, in_=in_), (), writes, key=key)

        def stor(out, in_, reads, key=None):
            P.dma("sp", lambda e: e.dma_start(out=out, in_=in_), reads, (), key=key)

        def wview(w):
            return w.rearrange("(kc p) f -> p kc f", p=128)

        def gcol(r, kc):
            return PVT[:, kc, r:r + 1]

        alt = [0]

        act_only = [False]

        def evac(out, in_, reads, writes):
            alt[0] ^= 1
            cp(out, in_, reads, writes, eng=("act" if (alt[0] or act_only[0]) else "dve"))

        b0 = Bump()
        vst = b0.alloc([1024], F32)
        xst = [b0.alloc([1024], F32) for _ in range(3)]
        ld(ident[:, :], ident_d, ["ident"])
        ld(vst[0:64, :], vecs, ["vst"])
        cp(identb[:, :], ident[:, :], ["ident"], ["identb"])
        memset(onesb[:, :], 1.0 / 1024.0, ["onesb"])
        memset(ones512[:, :], 1.0 / 512.0, ["ones512"])
        for kc in range(8):
            tr(bank(7)[:, 64 * kc:64 * kc + 64], vst[0:64, 128 * kc:128 * kc + 128], ident[0:64, 0:64],
               ["vst", "ident"], ["b7"])
        cp(PVT[:, :, :], bank(7).rearrange("p (k r) -> p k r", r=64), ["b7"], ["PVT"])
        for i in range(17):
            src = xp[128 * i:128 * i + 128, :] if i < 16 else xs
            slot = i % 3
            ld(xst[slot][:, :], src, ["xst%d" % slot])
            for h in range(2):
                bk = 2 * (i % 2) + h
                for k4 in range(4):
                    kc = 4 * h + k4
                    tr(bank(bk)[:, 128 * k4:128 * k4 + 128], xst[slot][:, 128 * kc:128 * kc + 128], ident[:, :],
                       ["xst%d" % slot, "ident"], ["b%d" % bk])
                evac(xT[:, 4 * h:4 * h + 4, 128 * i:128 * i + 128], bank(bk).rearrange("p (k t) -> p k t", t=128),
                     ["b%d" % bk], ["x%d" % (i // 4)])
        P.barrier()

        chk(0)
        act_only[0] = True
        def norm_tile(t, grow, dst, dst_key, sq, rstd, psb, fp32_out=False):
            c0, N = TILES[t]
            xk = "x%d" % t
            act(sq[:, :, 0:N], xT[:, :, c0:c0 + N], AF.Square, [xk], ["sq"])
            for kc in range(8):
                mm(bank(psb)[:, 0:N], onesb[:, :], sq[:, kc, 0:N], kc == 0, kc == 7, ["onesb", "sq"], ["b%d" % psb])
            act(rstd[:, 0:N], bank(psb)[:, 0:N], AF.Sqrt, ["b%d" % psb], ["rstd"], bias=EPS, scale=1.0)
            recip(rstd[:, 0:N], rstd[:, 0:N], ["rstd"], ["rstd"])
            for kc in range(8):
                stt(dst(kc, N), xT[:, kc, c0:c0 + N], gcol(grow, kc), rstd[:, 0:N], ALU.mult, ALU.mult,
                    [xk, "PVT", "rstd"], [dst_key])

        win_fixed = Bump(28000).alloc([8, 1536], BF16)

        def ffn(grow, wgate, wup, wdown, tag, tail_cb=None, extra=None):
            bp = Bump()
            xn = bp.alloc([8, NT], BF16)
            wgs = [bp.alloc([8, 512], BF16) for _ in range(2)]
            wus = [bp.alloc([8, 512], BF16) for _ in range(2)]
            wds = [bp.alloc([4, 1024], BF16) for _ in range(2)]
            hT = [bp.alloc([4, 512], BF16) for _ in range(2)]
            sg = [bp.alloc([512], F32) for _ in range(2)]
            sq = bp.alloc([8, 512], BF16)
            rstd = bp.alloc([512], F32)
            groups = [(0, 2), (2, 4), (6, 4), (10, 4), (14, 4), (18, 4)]

            def load_w(gi):
                s = gi % 2
                f0, G = groups[gi]
                ld(wgs[s][:, :, 0:128 * G], wview(wgate)[:, :, 128 * f0:128 * (f0 + G)], ["wg%d" % s], eng="pool")
                ld(wus[s][:, :, 0:128 * G], wview(wup)[:, :, 128 * f0:128 * (f0 + G)], ["wu%d" % s], eng="pool")
                ld(wds[s][:, 0:G, :], wdown[128 * f0:128 * (f0 + G), :].rearrange("(g p) o -> p g o", p=128),
                   ["wd%d" % s], eng="pool")

            load_w(0)
            load_w(1)
            if tag == "f1":
                ld(win_fixed[:, :, :], wview(w_in), ["win"], eng="pool")
            if extra is not None:
                extra(bp, sq, rstd)
            def do_norm(t):
                c0 = TILES[t][0]
                norm_tile(t, grow, lambda kc, N, c0=c0: xn[:, kc, c0:c0 + N], "xn%d" % t, sq, rstd, 6)

            do_norm(0)
            do_norm(1)
            cnt = [0, 0, 0]

            def GU(gi, t):
                s = gi % 2
                c0, N = TILES[t]
                hs = cnt[1] % 2
                for g in range(groups[gi][1]):
                    a = cnt[0] % 2
                    cnt[0] += 1
                    for kc in range(8):
                        mm(bank(a)[:, 0:N], wgs[s][:, kc, 128 * g:128 * g + 128], xn[:, kc, c0:c0 + N], kc == 0, kc == 7,
                           ["wg%d" % s, "xn%d" % t], ["b%d" % a])
                    for kc in range(8):
                        mm(bank(2 + a)[:, 0:N], wus[s][:, kc, 128 * g:128 * g + 128], xn[:, kc, c0:c0 + N], kc == 0, kc == 7,
                           ["wu%d" % s, "xn%d" % t], ["b%d" % (2 + a)])
                    act(sg[a][:, 0:N], bank(a)[:, 0:N], AF.Silu, ["b%d" % a], ["sg%d" % a])
                    tt(hT[hs][:, g, 0:N], sg[a][:, 0:N], bank(2 + a)[:, 0:N], ALU.mult,
                       ["sg%d" % a, "b%d" % (2 + a)], ["hT%d" % hs])
                cnt[1] += 1
                return hs

            def D(gi, t, hs):
                s = gi % 2
                c0, N = TILES[t]
                G = groups[gi][1]
                for oc in range(8):
                    a = 4 + cnt[2] % 2
                    cnt[2] += 1
                    for g in range(G):
                        mm(bank(a)[:, 0:N], wds[s][:, g, 128 * oc:128 * oc + 128], hT[hs][:, g, 0:N], g == 0, g == G - 1,
                           ["wd%d" % s, "hT%d" % hs], ["b%d" % a])
                    stt(xT[:, oc, c0:c0 + N], bank(a)[:, 0:N], 0.5, xT[:, oc, c0:c0 + N], ALU.mult, ALU.add,
                        ["b%d" % a, "x%d" % t], ["x%d" % t])

            prev = None
            for gi in range(6):
                for t in range(5):
                    hs = GU(gi, t)
                    if gi == 0 and t + 2 < 5:
                        do_norm(t + 2)
                    if prev is not None:
                        D(*prev)
                        if prev[1] == 4 and prev[0] + 2 < 6:
                            load_w(prev[0] + 2)
                        if prev[0] == 5 and tail_cb is not None:
                            tail_cb(prev[1])
                    prev = (gi, t, hs)
            D(*prev)
            if tail_cb is not None:
                tail_cb(prev[1])
                tail_cb(None)
            P.barrier()

        ffn(0, w_gate1, w_up1, w_down1, "f1")
        if "x1" in dbg:
            stor(dbg_out["x1"], xT[:, :, :], ["x0", "x1", "x2", "x3", "x4"], key="dbgx1")
            P.barrier()

        chk(1)
        pm = Bump()
        ub = pm.alloc([4, NT], BF16)
        base_u = pm.off
        VB = pm.alloc([4, 30 + 2048], BF16)
        VS = pm.alloc([4, 16, 38], BF16)
        vtail = pm.alloc([4, 30], F32)
        vnew = pm.alloc([4, 128], F32)
        base_m = pm.off

        DG = Bump(19100).alloc([4, 31, 128], BF16)
        assert 19100 + 7936 <= 28000
        bp = Bump(base_m)
        win = win_fixed
        xnt2 = [bp.alloc([8, 512], BF16) for _ in range(2)]
        sq = bp.alloc([8, 512], BF16)
        rstd = bp.alloc([512], F32)
        sgm = [bp.alloc([512], F32) for _ in range(2)]
        cst = [bp.alloc([512], F32) for _ in range(2)]
        memset(VB[:, :, 0:30], 0.0, ["VB"])
        for i in range(4):
            s = i % 2
            ld(cst[s][0:120, :], cconv[4 * i:4 * i + 4, :, :].rearrange("s t c -> (s t) c"), ["cst%d" % s])
            for q in range(4):
                tr(bank(7)[:, 120 * q:120 * q + 120], cst[s][0:120, 128 * q:128 * q + 128], ident[0:120, 0:120],
                   ["cst%d" % s, "ident"], ["b7"])
            cp(VS[:, :, 4 * i:4 * i + 4, 0:30], bank(7)[:, 0:480].rearrange("p (q s t) -> p q s t", q=4, s=4),
               ["b7"], ["VS"], eng="act")
        P.dma("sp", lambda e: e.dma_start(out=nconv_s[:, 0:22, :], in_=cconv[:, 8:30, :]), (), (), key="nconv_copy")
        k1 = [0]

        def m1_norm(t):
            xb = xnt2[t % 2]
            norm_tile(t, 1, lambda kc, N, xb=xb: xb[:, kc, 0:N], "xnt%d" % (t % 2), sq, rstd, 6)

        m1_norm(0)
        for t in range(5):
            c0, N = TILES[t]
            xnt = xnt2[t % 2]
            xk_ = "xnt%d" % (t % 2)
            if t + 1 < 5:
                m1_norm(t + 1)
            dgq = list(range(31)) if t < 4 else []

            def dg_some(n):
                for _ in range(n):
                    if dgq:
                        k = dgq.pop(0)
                        act(DG[:, t, k, :], identb[:, :], AF.Identity, ["identb", "PVT"], ["DG%d" % t], scale=gcol(8 + k, t))
            for q in range(4):
                a = k1[0] % 2
                k1[0] += 1
                for kc in range(8):
                    mm(bank(a)[:, 0:N], win[:, kc, 128 * q:128 * q + 128], xnt[:, kc, 0:N], kc == 0, kc == 7,
                       ["win", xk_], ["b%d" % a])
                cp(ub[:, q, c0:c0 + N], bank(a)[:, 0:N], ["b%d" % a], ["ub%d" % t], eng="act")
                dg_some(4)
            for q in range(4):
                a = k1[0] % 2
                k1[0] += 1
                for kc in range(8):
                    mm(bank(2 + a)[:, 0:N], win[:, kc, 512 + 128 * q:512 + 128 * q + 128], xnt[:, kc, 0:N], kc == 0, kc == 7,
                       ["win", xk_], ["b%d" % (2 + a)])
                for kc in range(8):
                    mm(bank(4 + a)[:, 0:N], win[:, kc, 1024 + 128 * q:1024 + 128 * q + 128], xnt[:, kc, 0:N], kc == 0, kc == 7,
                       ["win", xk_], ["b%d" % (4 + a)])
                act(sgm[a][:, 0:N], bank(4 + a)[:, 0:N], AF.Sigmoid, ["b%d" % (4 + a)], ["sgm%d" % a])
                if t < 4:
                    tt(VB[:, q, 30 + c0:30 + c0 + N], bank(2 + a)[:, 0:N], sgm[a][:, 0:N], ALU.mult,
                       ["b%d" % (2 + a), "sgm%d" % a], ["VB"])
                    if t == 3:
                        tt(vtail[:, q, :], bank(2 + a)[:, 482:512], sgm[a][:, 482:512], ALU.mult,
                           ["b%d" % (2 + a), "sgm%d" % a], ["vtail"])
                else:
                    tt(vnew[:, q, :], bank(2 + a)[:, 0:N], sgm[a][:, 0:N], ALU.mult,
                       ["b%d" % (2 + a), "sgm%d" % a], ["vnew"])
                    cp(VS[:, q, :, 30:38], vnew[:, q, :].rearrange("p (s t) -> p s t", t=8), ["vnew"], ["VS"])
                dg_some(4)
            dg_some(31)
        P.barrier()

        chk(2)
        pf = Bump(32000)
        Bre = pf.alloc([16, 32], F32); Bim = pf.alloc([16, 32], F32)
        ZnR = pf.alloc([4, 128], F32); ZnI = pf.alloc([4, 128], F32)
        LR = pf.alloc([16], F32); LI = pf.alloc([16], F32); DT = pf.alloc([16], F32)
        AR = pf.alloc([16], F32); AI = pf.alloc([16], F32)
        tA = pf.alloc([16], F32); tB = pf.alloc([16], F32); tC = pf.alloc([16], F32); tI = pf.alloc([16], I32)
        CRE = pf.alloc([16], F32); CIM = pf.alloc([16], F32); DECs = pf.alloc([16], F32)
        PWr = pf.alloc([9, 16], F32); PWi = pf.alloc([9, 16], F32)
        assert pf.off <= AW
        S5K = ["s5t"]
        chain_ops = []

        def record_s5_chain():
            saved = P.ops
            P.ops = []
            act(DT[:, :], DT[:, :], AF.Exp, ["DT"], S5K)
            tt(tA[:, :], LR[:, :], DT[:, :], ALU.mult, ["LR"] + S5K, S5K)
            act(tB[:, :], tA[:, :], AF.Exp, S5K, S5K)
            act(DECs[:, :], tA[:, :], AF.Exp, S5K, S5K, scale=8.0)
            tt(tA[:, :], LI[:, :], DT[:, :], ALU.mult, ["LI"] + S5K, S5K)

            def sincos(dst, shift):
                TWO_PI = 2.0 * math.pi
                ts(tC[:, :], tA[:, :], 8.0 * math.pi + shift, 1.0 / TWO_PI, ALU.add, ALU.mult, S5K, S5K)
                cp(tI[:, :], tC[:, :], S5K, S5K)
                cp(dst, tI[:, :], S5K, S5K)
                tt(dst, tC[:, :], dst, ALU.subtract, S5K, S5K)
                ts(tC[:, :], dst, 0.5, -1.0, ALU.is_gt, ALU.mult, S5K, S5K)
                tt(dst, dst, tC[:, :], ALU.add, S5K, S5K)
                ts(dst, dst, TWO_PI, math.pi, ALU.mult, ALU.min, S5K, S5K)
                ts1(dst, dst, -math.pi, ALU.max, S5K, S5K)
                act(dst, dst, AF.Sin, S5K, S5K)

            sincos(AI[:, :], 0.0)
            sincos(AR[:, :], 0.5 * math.pi)
            tt(AR[:, :], AR[:, :], tB[:, :], ALU.mult, S5K, S5K)
            tt(AI[:, :], AI[:, :], tB[:, :], ALU.mult, S5K, S5K)
            tt(tA[:, :], LR[:, :], LR[:, :], ALU.mult, ["LR"] + S5K, S5K)
            tt(tB[:, :], LI[:, :], LI[:, :], ALU.mult, ["LI"] + S5K, S5K)
            tt(tA[:, :], tA[:, :], tB[:, :], ALU.add, S5K, S5K)
            recip(tA[:, :], tA[:, :], S5K, S5K)
            ts1(tB[:, :], AR[:, :], -1.0, ALU.add, S5K, S5K)
            tt(CRE[:, :], tB[:, :], LR[:, :], ALU.mult, S5K, S5K)
            tt(tC[:, :], AI[:, :], LI[:, :], ALU.mult, S5K, S5K)
            tt(CRE[:, :], CRE[:, :], tC[:, :], ALU.add, S5K, S5K)
            tt(CRE[:, :], CRE[:, :], tA[:, :], ALU.mult, S5K, S5K)
            tt(CIM[:, :], AI[:, :], LR[:, :], ALU.mult, S5K, S5K)
            tt(tC[:, :], tB[:, :], LI[:, :], ALU.mult, S5K, S5K)
            tt(CIM[:, :], CIM[:, :], tC[:, :], ALU.subtract, S5K, S5K)
            tt(CIM[:, :], CIM[:, :], tA[:, :], ALU.mult, S5K, S5K)
            memset(PWr[:, 0, :], 1.0, S5K); memset(PWi[:, 0, :], 0.0, S5K)
            cp(PWr[:, 1, :], AR[:, :], S5K, S5K); cp(PWi[:, 1, :], AI[:, :], S5K, S5K)
            for k in range(1, 8):
                tt(tA[:, :], PWr[:, k, :], AR[:, :], ALU.mult, S5K, S5K)
                tt(tB[:, :], PWi[:, k, :], AI[:, :], ALU.mult, S5K, S5K)
                tt(PWr[:, k + 1, :], tA[:, :], tB[:, :], ALU.subtract, S5K, S5K)
                tt(tA[:, :], PWr[:, k, :], AI[:, :], ALU.mult, S5K, S5K)
                tt(tB[:, :], PWi[:, k, :], AR[:, :], ALU.mult, S5K, S5K)
                tt(PWi[:, k + 1, :], tA[:, :], tB[:, :], ALU.add, S5K, S5K)

            chain_ops.extend(P.ops)
            P.ops = saved

        def sprinkle(n):
            P.ops.extend(chain_ops[:n])
            del chain_ops[:n]

        def s5_prefetch():
            for e_ in range(2):
                lo, hi = 64 * e_, 64 * e_ + 64
                ld(LR[lo:hi, :], a_re.rearrange("(P e) n -> e n P", e=2)[e_], ["LR"])
                ld(LI[lo:hi, :], a_im.rearrange("(P e) n -> e n P", e=2)[e_], ["LI"])
                ld(DT[lo:hi, :], log_dt.rearrange("o (P e) -> e (o P)", e=2)[e_:e_ + 1, :].to_broadcast([64, 16]), ["DT"])
            memset(Bre[:, :, :], 0.0, ["Bre"]); memset(Bim[:, :, :], 0.0, ["Bim"])
            for e_ in range(2):
                lo, hi = 64 * e_, 64 * e_ + 64
                ld(Bre[lo:hi, :, 16 * e_:16 * e_ + 16], b_re.rearrange("(P e) n c -> e n P c", e=2)[e_], ["Bre"])
                ld(Bim[lo:hi, :, 16 * e_:16 * e_ + 16], b_im.rearrange("(P e) n c -> e n P c", e=2)[e_], ["Bim"])
            for (csrc, Zn_, nm) in ((c_re, ZnR, "ZnR"), (c_im, ZnI, "ZnI")):
                for d_ in range(2):
                    ld(Zn_[:, :, 64 * d_:64 * d_ + 64], csrc.rearrange("(t g) c n -> (g c) t n", g=8), [nm])

        bp = Bump(base_m)
        wol = bp.alloc([4, 1024], BF16)
        yc = bp.alloc([4, 512], F32)
        ycb = bp.alloc([4, 512], BF16)
        sqb = bp.alloc([4, 512], BF16)
        m2 = bp.alloc([512], F32)
        rs = bp.alloc([512], F32)
        nmr = bp.alloc([512], F32)
        assert bp.off <= 19100, bp.off
        bp = Bump(19100 + 7936)
        tq = [bp.alloc([512], F32) for _ in range(2)]
        sgq = [bp.alloc([512], F32) for _ in range(2)]
        yn_ = [bp.alloc([512], F32) for _ in range(2)]
        cout = bp.alloc([4, 512], BF16)
        ost = bp.alloc([512], F32)
        ld(wol[:, :, :], w_out[512:1024, :].rearrange("(kc p) f -> p kc f", p=128), ["wol"], eng="pool")
        for q in range(4):
            tr(bank(7)[0:30, 128 * q:128 * q + 128], vtail[:, q, :], ident[:, :], ["vtail", "ident"], ["b7"])
        cp(ost[0:30, :], bank(7)[0:30, :], ["b7"], ["ost"])
        stor(nconv_p, ost[0:30, :], ["ost"])
        for q in range(4):
            tr(bank(6)[:, 128 * q:128 * q + 128], vnew[:, q, :], ident[:, :], ["vnew", "ident"], ["b6"])
        cp(yc[:, 0, :], bank(6)[:, :], ["b6"], ["yc0"])
        for s in range(16):
            stor(nconv_s[s, 22:30, :], yc[8 * s:8 * s + 8, 0, :], ["yc0"], key="ycst")

        def conv_stage(t):
            c0, N = TILES[t]
            for q in range(4):
                for k in range(31):
                    if t < 4:
                        rhs = VB[:, q, c0 + k:c0 + k + 512]
                    else:
                        rhs = VS[:, q, :, k:k + 8]
                    mm(bank(q)[:, 0:N], DG[:, q, k, :], rhs, k == 0, k == 30, ["DG%d" % q, "VB", "VS"], ["b%d" % q])
                act(yc[:, q, 0:N], bank(q)[:, 0:N], AF.Identity, ["b%d" % q, "PVT"], ["yc%d" % q], bias=gcol(6, 4 + q), scale=1.0)
                cp(ycb[:, q, 0:N], yc[:, q, 0:N], ["yc%d" % q], ["ycb"], eng="act")
                act(sqb[:, q, 0:N], yc[:, q, 0:N], AF.Square, ["yc%d" % q], ["sqb"])

        def ln_stage(t):
            c0, N = TILES[t]
            for q in range(4):
                mm(bank(4)[:, 0:N], ones512[:, :], ycb[:, q, 0:N], q == 0, q == 3, ["ones512", "ycb"], ["b4"])
            for q in range(4):
                mm(bank(5)[:, 0:N], ones512[:, :], sqb[:, q, 0:N], q == 0, q == 3, ["ones512", "sqb"], ["b5"])
            act(m2[:, 0:N], bank(4)[:, 0:N], AF.Square, ["b4"], ["m2"])
            tt(rs[:, 0:N], bank(5)[:, 0:N], m2[:, 0:N], ALU.subtract, ["b5", "m2"], ["rs"])
            ts(rs[:, 0:N], rs[:, 0:N], 0.0, EPS, ALU.max, ALU.add, ["rs"], ["rs"])
            act(rs[:, 0:N], rs[:, 0:N], AF.Sqrt, ["rs"], ["rs"])
            recip(rs[:, 0:N], rs[:, 0:N], ["rs"], ["rs"])
            stt(nmr[:, 0:N], bank(4)[:, 0:N], -1.0, rs[:, 0:N], ALU.mult, ALU.mult, ["b4", "rs"], ["nmr"])
            for q in range(4):
                a = q % 2
                tt(tq[a][:, 0:N], yc[:, q, 0:N], rs[:, 0:N], ALU.mult, ["yc%d" % q, "rs"], ["tq%d" % a])
                tt(tq[a][:, 0:N], tq[a][:, 0:N], nmr[:, 0:N], ALU.add, ["tq%d" % a, "nmr"], ["tq%d" % a])
                act(sgq[a][:, 0:N], tq[a][:, 0:N], AF.Sigmoid, ["tq%d" % a, "PVT"], ["sgq%d" % a], scale=gcol(7, q), bias=gcol(7, 4 + q))
                act(yn_[a][:, 0:N], tq[a][:, 0:N], AF.Identity, ["tq%d" % a, "PVT"], ["yn%d" % a], scale=gcol(7, q), bias=gcol(7, 4 + q))
                tt(cout[:, q, 0:N], yn_[a][:, 0:N], sgq[a][:, 0:N], ALU.mult, ["yn%d" % a, "sgq%d" % a], ["cout"])
                if t >= 1:
                    sprinkle(4)

        def wout_lo_stage(t):
            c0, N = TILES[t]
            for oc in range(8):
                a = 6 + oc % 2
                for kc in range(4):
                    mm(bank(a)[:, 0:N], wol[:, kc, 128 * oc:128 * oc + 128], cout[:, kc, 0:N], kc == 0, kc == 3,
                       ["wol", "cout"], ["b%d" % a])
                tt(xT[:, oc, c0:c0 + N], bank(a)[:, 0:N], xT[:, oc, c0:c0 + N], ALU.add, ["b%d" % a, "x%d" % t], ["x%d" % t])
                if t >= 1:
                    sprinkle(2)

        conv_stage(0)
        s5_prefetch()
        record_s5_chain()
        for t in range(5):
            ln_stage(t)
            if t + 1 < 5:
                conv_stage(t + 1)
            wout_lo_stage(t)
        sprinkle(len(chain_ops))
        P.barrier()
        chk(3)
        bp = Bump(base_u)
        KL = bp.alloc([4, 8, 128], BF16)
        WXS = bp.alloc([4, 8, 2, 128], BF16)
        WY = bp.alloc([16, 8, 2, 32], BF16)
        wglu = bp.alloc([4, 512], BF16)
        woh = bp.alloc([4, 1024], BF16)
        T1 = bp.alloc([2, 16], F32)
        T2 = bp.alloc([2, 16], F32)
        DEC = bp.alloc([16], F32)
        C64 = bp.alloc([16, 64], F32)
        S64 = bp.alloc([16, 64], F32)
        base_rt = bp.off
        ld(wglu[:, :, :], w_glu.rearrange("(kc p) f -> p kc f", p=128), ["wglu"], eng="pool")
        ld(woh[:, :, :], w_out[0:512, :].rearrange("(kc p) f -> p kc f", p=128), ["woh"], eng="pool")
        BBr = bp.alloc([16, 32], F32); BBi = bp.alloc([16, 32], F32)
        Cre = bp.alloc([16, 32], F32); Cim = bp.alloc([16, 32], F32); CimN = bp.alloc([16, 32], F32)
        Xr = bp.alloc([16, 32], F32); Xi = bp.alloc([16, 32], F32)
        Xr2 = bp.alloc([16, 32], F32); Xi2 = bp.alloc([16, 32], F32)
        Xb = [[bp.alloc([16, 32], BF16) for _ in range(2)] for _ in range(2)]
        Cbr = bp.alloc([16, 32], BF16); Cbi = bp.alloc([16, 32], BF16)
        u1 = bp.alloc([16, 32], F32); u2 = bp.alloc([16, 32], F32); u3 = bp.alloc([16, 32], F32); u4 = bp.alloc([16, 32], F32)
        ucnt = [0]
        memset(Cre[:, :, :], 0.0, ["Cre"]); memset(Cim[:, :, :], 0.0, ["Cim"])
        for (Zn, znk, cdst, nm) in ((ZnR, "ZnR", Cre, "Cre"), (ZnI, "ZnI", Cim, "Cim")):
            for T_ in range(4):
                tr(bank(7)[:, 128 * T_:128 * T_ + 128], Zn[:, T_, :], ident[:, :], [znk, "ident"], ["b7"])
            for e_ in range(2):
                lo, hi = 64 * e_, 64 * e_ + 64
                cp(cdst[lo:hi, :, 16 * e_:16 * e_ + 16].rearrange("p (t l) c -> p t l c", t=4),
                   bank(7)[lo:hi, :].rearrange("p (t l e c) -> p t l e c", t=4, l=4, e=2)[:, :, :, e_, :], ["b7"], [nm], eng="act")
        act(CimN[:, :, :], Cim[:, :, :], AF.Copy, ["Cim"], ["CimN"], scale=-1.0)
        cp(Cbr[:, :, :], Cre[:, :, :], ["Cre"], ["Cbr"], eng="act")
        cp(Cbi[:, :, :], CimN[:, :, :], ["CimN"], ["Cbi"], eng="act")
        cp(DEC[:, :], DECs[:, :], S5K, ["DEC"])
        cp(T1[:, 0, :], PWr[:, 8, :], S5K, ["T12"]); cp(T1[:, 1, :], PWr[:, 8, :], S5K, ["T12"])
        ts1(T2[:, 0, :], PWi[:, 8, :], -1.0, ALU.mult, S5K, ["T12"]); cp(T2[:, 1, :], PWi[:, 8, :], S5K, ["T12"])

        Ec = bp.alloc([16], F32); Es = bp.alloc([16], F32); r1 = bp.alloc([16, 32], F32); r2 = bp.alloc([16, 32], F32)
        recip(tA[:, :], DEC[:, :], ["DEC"], S5K)
        tt(Ec[:, :], PWr[:, 8, :], tA[:, :], ALU.mult, S5K, ["E"])
        tt(Es[:, :], PWi[:, 8, :], tA[:, :], ALU.mult, S5K, ["E"])
        memset(C64[:, :, 0:1], 1.0, ["CS"]); memset(S64[:, :, 0:1], 0.0, ["CS"])
        m_ = 1
        while m_ < 64:
            bcm = lambda v: v.unsqueeze(2).to_broadcast([128, 16, m_])
            a1 = r1[:, :, 0:m_]; a2 = r2[:, :, 0:m_]
            tt(a1, C64[:, :, 0:m_], bcm(Ec[:, :]), ALU.mult, ["CS", "E"], ["r1"])
            tt(a2, S64[:, :, 0:m_], bcm(Es[:, :]), ALU.mult, ["CS", "E"], ["r2"])
            tt(C64[:, :, m_:2 * m_], a1, a2, ALU.subtract, ["r1", "r2"], ["CS"])
            tt(a1, S64[:, :, 0:m_], bcm(Ec[:, :]), ALU.mult, ["CS", "E"], ["r1"])
            tt(a2, C64[:, :, 0:m_], bcm(Es[:, :]), ALU.mult, ["CS", "E"], ["r2"])
            tt(S64[:, :, m_:2 * m_], a1, a2, ALU.add, ["r1", "r2"], ["CS"])
            if 2 * m_ < 64:
                tt(tA[:, :], Ec[:, :], Ec[:, :], ALU.mult, ["E"], S5K)
                tt(tB[:, :], Es[:, :], Es[:, :], ALU.mult, ["E"], S5K)
                tt(tC[:, :], Ec[:, :], Es[:, :], ALU.mult, ["E"], S5K)
                tt(Ec[:, :], tA[:, :], tB[:, :], ALU.subtract, S5K, ["E"])
                ts1(Es[:, :], tC[:, :], 2.0, ALU.mult, S5K, ["E"])
            m_ *= 2

        def bc(v16):
            return v16.unsqueeze(2).to_broadcast([128, 16, 32])

        def cmul(dr, di, ar, ai, br, bi, rk, wk, neg_im=False):
            wr = wk if isinstance(wk, tuple) else (wk, wk)
            tt(u1[:, :, :], br, bc(ar), ALU.mult, rk, ["u1"])
            tt(u2[:, :, :], bi, bc(ai), ALU.mult, rk, ["u2"])
            tt(u3[:, :, :], bi, bc(ar), ALU.mult, rk, ["u3"])
            tt(u4[:, :, :], br, bc(ai), ALU.mult, rk, ["u4"])
            tt(dr, u1[:, :, :], u2[:, :, :], ALU.subtract, ["u1", "u2"], wr[0])
            if neg_im:
                stt(di, u3[:, :, :], -1.0, u4[:, :, :], ALU.mult, ALU.subtract, ["u3", "u4"], wr[1])
            else:
                tt(di, u3[:, :, :], u4[:, :, :], ALU.add, ["u3", "u4"], wr[1])

        cmul(BBr[:, :, :], BBi[:, :, :], CRE[:, :], CIM[:, :], Bre[:, :, :], Bim[:, :, :], S5K + ["Bre", "Bim"], ["BB"])
        memset(KL[:, :, :, :], 0.0, ["KL"])
        def pc_A(k):
            cmul(Xb[k % 2][0][:, :, :], Xb[k % 2][1][:, :, :], PWr[:, k, :], PWi[:, k, :], BBr[:, :, :], BBi[:, :, :],
                 S5K + ["BB"], ["X%d" % (k % 2)])

        def pc_B(k):
            Xr_, Xi_ = Xb[k % 2]
            xk = "X%d" % (k % 2)
            kb = 6 + k % 2
            for q in range(4):
                for pl in range(4):
                    Pp = 4 * q + pl
                    o = bank(kb)[32 * pl:32 * pl + 32, 128 * q + 32 * pl:128 * q + 32 * pl + 32]
                    mm(o, Xr_[:, Pp, :], Cbr[:, Pp, :], True, False, [xk, "Cbr"], ["b%d" % kb], tile_position=(0, 32 * pl))
                    mm(o, Xi_[:, Pp, :], Cbi[:, Pp, :], False, True, [xk, "Cbi"], ["b%d" % kb], tile_position=(0, 32 * pl))
            for ri, X_ in enumerate((Xr_, Xi_)):
                bk = 2 * (k % 2) + ri
                for q in range(4):
                    tr(bankb(bk)[:, 128 * q:128 * q + 128], X_[:, 4 * q:4 * q + 4, :].rearrange("p a b -> p (a b)"), identb[:, :],
                       [xk, "identb"], ["b%d" % bk])

        def pc_C(k):
            kb = 6 + k % 2
            for b in range(4):
                cp(KL[32 * b:32 * b + 32, :, k, 32 * b:32 * b + 32],
                   bank(kb)[32 * b:32 * b + 32, :].rearrange("p (q l c) -> p q l c", q=4, l=4)[:, :, b, :], ["b%d" % kb], ["KL"],
                   eng="act")
            for ri in range(2):
                bk = 2 * (k % 2) + ri
                cp(WXS[:, :, 7 - k, ri, :], bankb(bk)[:, 0:512].rearrange("p (q m) -> p q m", q=4), ["b%d" % bk], ["WXS"], eng="act")

        def pc_D(k):
            cmul(WY[:, :, k, 0, :], WY[:, :, k, 1, :], PWr[:, k + 1, :], PWi[:, k + 1, :], Cre[:, :, :], Cim[:, :, :],
                 S5K + ["Cre", "Cim"], ["WY"], neg_im=True)

        pc_A(0)
        for k in range(8):
            pc_B(k)
            if k + 1 < 8:
                pc_A(k + 1)
            pc_D(k)
            pc_C(k)
        P.barrier()
        for nm_, ap_ in (("AR", AR), ("AI", AI), ("CRE", CRE), ("CIM", CIM), ("PWr", PWr), ("PWi", PWi), ("BBr", BBr), ("BBi", BBi),
                         ("Cre", Cre), ("Cim", Cim), ("Bre", Bre), ("LR", LR), ("LI", LI), ("DT", DT), ("T1", T1), ("T2", T2)):
            dump(nm_, ap_[:])
        dump("KL", KL[:], BF16); dump("WXS", WXS[:], BF16); dump("WY", WY[:], BF16)
        chk(3.5)
        bp = Bump(base_rt)
        XS = bp.alloc([2, 16, 64], F32)
        RR = bp.alloc([2, 16, 64], F32)
        CA = bp.alloc([3, 16], F32)
        r1 = bp.alloc([16, 64], F32); r2 = bp.alloc([16, 64], F32)
        Hb = bp.alloc([2, 16, 64], BF16)
        H0 = bp.alloc([2, 16, 16], F32)
        fin = r2[:, 0:8, :].rearrange("p a b -> p (a b)").rearrange("p (r a b) -> p r a b", r=2, a=16)
        h0st = RR[:, :, :, :].rearrange("p a b c -> p (a b c)")
        m1 = bp.alloc([2, 16], F32); m2_ = bp.alloc([2, 16], F32)
        yt = [bp.alloc([512], F32) for _ in range(2)]
        inn = [bp.alloc([512], F32) for _ in range(2)]
        sgg = [bp.alloc([512], F32) for _ in range(2)]
        gf = bp.alloc([4, 512], F32)
        gb = bp.alloc([4, 512], BF16)
        so = bp.alloc([4, 512], BF16)
        memset(CA[:, :, :], 0.0, ["CA"])
        def load_h0():
            for ri, src_ in enumerate((st_re, st_im)):
                ld(h0st[0:16, :], src_, ["h0st"])
                for Pp in range(16):
                    tr(bank(7)[:, 256 * ri + 16 * Pp:256 * ri + 16 * Pp + 16], h0st[0:16, 128 * Pp:128 * Pp + 128],
                       ident[0:16, 0:16], ["h0st", "ident"], ["b7"])
            cp(H0[:, :, :, :], bank(7).rearrange("p (r P j) -> p r P j", r=2, P=16), ["b7"], ["H0"])
            memset(RR[:, :, :, :], 0.0, ["RR0", "RR1", "h0st"])

        def s5_xs(t):
                c0, N = TILES[t]
                J = N // 8
                uk = "ub%d" % t
                for q in range(4):
                    for ri in range(2):
                        for tau in range(8):
                            for b in range(4):
                                o = bank(b).rearrange("p (r q j) -> p r q j", r=2, q=4)[:, ri, q, 0:J]
                                mm(o, WXS[32 * b:32 * b + 32, q, tau, ri, :], ub[32 * b:32 * b + 32, q, c0 + tau:c0 + N:8],
                                   tau == 0, tau == 7, ["WXS", uk], ["b%d" % b], tile_position=(32 * b, 0))

        def s5_scan(t):
                c0, N = TILES[t]
                J = N // 8
                uk = "ub%d" % t
                if t < 4:
                    for b in range(4):
                        cp(RR[:, :, b:16:4, :], bank(b).rearrange("p (r q j) -> p r q j", r=2, q=4),
                           ["b%d" % b], ["RR0", "RR1"], eng="act")
                    XSKw = ["XSb%d" % b for b in range(4)]
                    tt(r1[:, :, :], RR[:, 0, :, :], C64[:, :, :], ALU.mult, ["RR0", "CS"], ["r1", "r1b"])
                    tt(r2[:, :, :], RR[:, 1, :, :], S64[:, :, :], ALU.mult, ["RR1", "CS"], ["r2", "r2b"])
                    tt(XS[:, 0, :, :], r1[:, :, :], r2[:, :, :], ALU.add, ["r1", "r2", "r1b", "r2b"], XSKw + ["S0"])
                    tt(r1[:, :, :], RR[:, 1, :, :], C64[:, :, :], ALU.mult, ["RR1", "CS"], ["r1", "r1b"])
                    tt(r2[:, :, :], RR[:, 0, :, :], S64[:, :, :], ALU.mult, ["RR0", "CS"], ["r2", "r2b"])
                    tt(XS[:, 1, :, :], r1[:, :, :], r2[:, :, :], ALU.subtract, ["r1", "r2", "r1b", "r2b"], XSKw + ["S1"])
                    XSK = ["XSb%d" % b for b in range(4)]
                    if t > 0:
                        tt(m1[:, :, :], T2[:, :, :], CA[:, 1:3, :], ALU.mult, ["T12", "CA"], ["m1"])
                        tt(m2_[:, :, :], T1[:, :, :], CA[:, 0:2, :], ALU.mult, ["T12", "CA"], ["m2"])
                        tt(m1[:, :, :], m1[:, :, :], m2_[:, :, :], ALU.add, ["m1", "m2"], ["m1"])
                        tt(XS[:, :, :, 0], XS[:, :, :, 0], m1[:, :, :], ALU.add, XSK + ["m1"], XSK)
                    for ri in range(2):
                        for Pp in range(16):
                            P.op("dve", lambda e, ri=ri, Pp=Pp: e.tensor_tensor_scan(
                                out=RR[:, ri, Pp, :], data0=DEC[:, Pp:Pp + 1].to_broadcast([128, 64]), data1=XS[:, ri, Pp, :],
                                initial=0.0, op0=ALU.mult, op1=ALU.add), ["DEC", "XSb%d" % (Pp % 4)], ["RR%d" % ri])
                    cp(Hb[:, :, :, 0], CA[:, 0:2, :], ["CA"], ["Hb"], eng="act")
                    tt(r1[:, :, :], RR[:, 0, :, :], C64[:, :, :], ALU.mult, ["RR0", "CS"], ["r1", "r1b"])
                    tt(r2[:, :, :], RR[:, 1, :, :], S64[:, :, :], ALU.mult, ["RR1", "CS"], ["r2", "r2b"])
                    tt(XS[:, 0, :, :], r1[:, :, :], r2[:, :, :], ALU.subtract, ["r1", "r2", "r1b", "r2b"] + XSK, ["S0"])
                    tt(r1[:, :, :], RR[:, 1, :, :], C64[:, :, :], ALU.mult, ["RR1", "CS"], ["r1", "r1b"])
                    tt(r2[:, :, :], RR[:, 0, :, :], S64[:, :, :], ALU.mult, ["RR0", "CS"], ["r2", "r2b"])
                    tt(XS[:, 1, :, :], r1[:, :, :], r2[:, :, :], ALU.add, ["r1", "r2", "r1b", "r2b"] + XSK, ["S1"] + XSK)
                    cp(Hb[:, 0, :, 1:64], XS[:, 0, :, 0:63], ["S0"], ["Hb"], eng="act")
                    cp(Hb[:, 1, :, 1:64], XS[:, 1, :, 0:63], ["S1"] + XSK, ["Hb"], eng="act")
                    cp(CA[:, 0:2, :], XS[:, :, :, 63], ["S0", "S1"] + XSK, ["CA"])
                    cp(CA[:, 2, :], XS[:, 0, :, 63], ["S0"], ["CA"])
                    if t == 3:
                        for ri in range(2):
                            tr(bank(0)[0:16, 128 * ri:128 * ri + 128], CA[:, ri, :], ident[:, :], ["CA", "ident"], ["b0"])
                        cp(fin[0:16, 0, :, :].rearrange("p a b -> p (a b)"), bank(0)[0:16, 0:256], ["b0"], ["fin", "r2", "r2b"])
                        stor(nre_p, fin[0:16, 0, 0:8, :].rearrange("p a b -> p (a b)"), ["fin"], key="finp")
                        stor(nim_p, fin[0:16, 0, 8:16, :].rearrange("p a b -> p (a b)"), ["fin"], key="finp")
                else:
                    for b in range(4):
                        cp(XS[:, :, b:16:4, 0:J], bank(b).rearrange("p (r q j) -> p r q j", r=2, q=4)[:, :, :, 0:J],
                           ["b%d" % b], ["XS", "S0", "S1", "XSb0", "XSb1", "XSb2", "XSb3"], eng="act")
                    cp(Hb[:, :, :, 0:16], H0[:, :, :, :], ["H0"], ["Hb"], eng="act")
                if t == 0:
                    dump("XS", XS[:]); dump("Hb", Hb[:], BF16)

        def s5_y(t):
                c0, N = TILES[t]
                J = N // 8
                uk = "ub%d" % t
                for q in range(4):
                    for k in range(8):
                        mm(bank(4 + q)[:, 0:N].rearrange("p (j t) -> p j t", t=8)[:, :, k:8], KL[:, q, k, :],
                           ub[:, q, c0:c0 + N].rearrange("p (j t) -> p j t", t=8)[:, :, 0:8 - k], k == 0, False,
                           ["KL", uk], ["b%d" % (4 + q)])
                for q in range(4):
                    for tau in range(8):
                        for ri in range(2):
                            for b in range(4):
                                Pp = 4 * q + b
                                mm(bank(4 + q)[32 * b:32 * b + 32, tau:N:8], WY[:, Pp, tau, ri, :], Hb[:, ri, Pp, 0:J], False,
                                   (tau == 7 and ri == 1), ["WY", "Hb"], ["b%d" % (4 + q)], tile_position=(0, 32 * b))

        def s5_epi(t):
                c0, N = TILES[t]
                J = N // 8
                uk = "ub%d" % t
                for q in range(4):
                    a = q % 2
                    stt(yt[a][:, 0:N], ub[:, q, c0:c0 + N], gcol(6, q), bank(4 + q)[:, 0:N], ALU.mult, ALU.add,
                        [uk, "PVT", "b%d" % (4 + q)], ["yt%d" % a])
                    act(inn[a][:, 0:N], yt[a][:, 0:N], AF.Square, ["yt%d" % a], ["inn%d" % a], scale=math.sqrt(0.044715))
                    stt(inn[a][:, 0:N], inn[a][:, 0:N], 1.0, yt[a][:, 0:N], ALU.add, ALU.mult, ["inn%d" % a, "yt%d" % a], ["inn%d" % a])
                    act(sgg[a][:, 0:N], inn[a][:, 0:N], AF.Sigmoid, ["inn%d" % a], ["sgg%d" % a], scale=2.0 * math.sqrt(2.0 / math.pi))
                    tt(gf[:, q, 0:N], yt[a][:, 0:N], sgg[a][:, 0:N], ALU.mult, ["yt%d" % a, "sgg%d" % a], ["gf"])
                    cp(gb[:, q, 0:N], gf[:, q, 0:N], ["gf"], ["gb"], eng="act")
                for oc in range(4):
                    a = 4 + oc % 2
                    for kc in range(4):
                        mm(bank(a)[:, 0:N], wglu[:, kc, 128 * oc:128 * oc + 128], gb[:, kc, 0:N], kc == 0, kc == 3,
                           ["wglu", "gb"], ["b%d" % a])
                    act(sgg[oc % 2][:, 0:N], bank(a)[:, 0:N], AF.Sigmoid, ["b%d" % a], ["sgg%d" % (oc % 2)])
                    tt(so[:, oc, 0:N], gf[:, oc, 0:N], sgg[oc % 2][:, 0:N], ALU.mult, ["gf", "sgg%d" % (oc % 2)], ["so"])
                for oc in range(8):
                    a = 6 + oc % 2
                    for kc in range(4):
                        mm(bank(a)[:, 0:N], woh[:, kc, 128 * oc:128 * oc + 128], so[:, kc, 0:N], kc == 0, kc == 3,
                           ["woh", "so"], ["b%d" % a])
                    tt(xT[:, oc, c0:c0 + N], bank(a)[:, 0:N], xT[:, oc, c0:c0 + N], ALU.add, ["b%d" % a, "x%d" % t], ["x%d" % t])
                if t == 4:
                    a8r = T1[:, 0, :].unsqueeze(2).to_broadcast([128, 16, 16])
                    a8i = T2[:, 1, :].unsqueeze(2).to_broadcast([128, 16, 16])
                    w1 = yt[0][:, 0:256].rearrange("p (a b) -> p a b", b=16)
                    w2 = yt[1][:, 0:256].rearrange("p (a b) -> p a b", b=16)
                    tt(w1, H0[:, 0, :, :], a8r, ALU.mult, ["H0", "T12"], ["yt0"])
                    tt(w2, H0[:, 1, :, :], a8i, ALU.mult, ["H0", "T12"], ["yt1"])
                    tt(w1, w1, w2, ALU.subtract, ["yt0", "yt1"], ["yt0"])
                    tt(fin[:, 0, :, :], w1, XS[:, 0, :, 0:16], ALU.add, ["yt0", "XS"], ["fin", "r2", "r2b"])
                    tt(w1, H0[:, 1, :, :], a8r, ALU.mult, ["H0", "T12"], ["yt0"])
                    tt(w2, H0[:, 0, :, :], a8i, ALU.mult, ["H0", "T12"], ["yt1"])
                    tt(w1, w1, w2, ALU.add, ["yt0", "yt1"], ["yt0"])
                    tt(fin[:, 1, :, :], w1, XS[:, 1, :, 0:16], ALU.add, ["yt0", "XS"], ["fin", "r2", "r2b"])
                    for ri, dst in enumerate((nre_s, nim_s)):
                        for Pp in range(16):
                            bk = Pp // 4
                            tr(bank(bk)[0:16, 128 * (Pp % 4):128 * (Pp % 4) + 128], fin[:, ri, Pp, :], ident[:, :],
                               ["fin", "ident"], ["b%d" % bk])
                        for bk in range(4):
                            cp(h0st[0:16, 512 * bk:512 * bk + 512], bank(bk)[0:16, :], ["b%d" % bk], ["h0st", "RR0", "RR1"])
                        stor(dst, h0st[0:16, :], ["h0st"], key="h0st_out")
        s5_xs(0)
        load_h0()
        s5_scan(0)
        s5_xs(1)
        for t in range(5):
            s5_y(t)
            if t + 1 < 5:
                s5_scan(t + 1)
            if t + 2 < 5:
                s5_xs(t + 2)
            s5_epi(t)
        P.barrier()

        if "x2" in dbg:
            stor(dbg_out["x2"], xT[:, :, :], ["x0", "x1", "x2", "x3", "x4"], key="dbgx2")
            P.barrier()
        chk(4)
        bp = Bump()
        wq = bp.alloc([8, 1024], BF16)
        wo = bp.alloc([8, 1024], BF16)
        KT = bp.alloc([8, 256], BF16)
        Vb = bp.alloc([2, 1024], BF16)
        mx = bp.alloc([4], F32); nb = bp.alloc([4], F32); ssum = bp.alloc([4], F32); rsm = bp.alloc([4], F32)
        base_a = bp.off
        xnt = bp.alloc([8, 512], BF16)
        sq = bp.alloc([8, 512], BF16)
        rstd = bp.alloc([512], F32)
        qT = [bp.alloc([8, 512], BF16) for _ in range(2)]
        qTs = bp.alloc([8, 128], BF16)
        base_b = bp.off
        wk = bp.alloc([8, 1024], BF16)
        wv = bp.alloc([8, 1024], BF16)
        mst = bp.alloc([1024], F32)
        mrb = bp.alloc([1024], BF16)
        mnT = bp.alloc([8, 256], BF16)
        kvo = [bp.alloc([1024], F32) for _ in range(2)]
        junk = bp.alloc([1024], BF16)
        ld(wq[:, :, :], wview(w_q), ["wq"], eng="pool")
        ld(wk[:, :, :], wview(w_mem_k), ["wk"], eng="pool")
        ld(wv[:, :, :], wview(w_mem_v), ["wv"], eng="pool")
        ld(wo[:, :, :], wview(w_o), ["wo"], eng="pool")
        bp = Bump(base_b)
        Pf = bp.alloc([4, 256], F32)
        Pb = bp.alloc([4, 256], BF16)
        PT = bp.alloc([2, 4, 512], BF16)
        oT = bp.alloc([8, 512], BF16)
        oTs = bp.alloc([8, 128], BF16)
        Ks = [bp.alloc([2, 1024], BF16) for _ in range(2)]
        NV = 4
        Vs = [bp.alloc([2, 1024], BF16) for _ in range(NV)]
        KTs = [bp.alloc([8, 256], BF16) for _ in range(2)]
        PTs = bp.alloc([2, 4, 128], BF16)
        m2x = bp.alloc([2], F32)
        SCALE = 1.0 / 16.0
        SETS = ((2, 3), (5, 7))

        def softmax_rows(st_):
            pa, pb_ = SETS[st_]
            for hh, bk in enumerate((pa, pb_)):
                P.op("dve", lambda e, hh=hh, bk=bk: e.tensor_reduce(
                    out=m2x[:, hh:hh + 1], in_=bank(bk), axis=mybir.AxisListType.X, op=ALU.max, negate=True),
                    ["b%d" % bk], ["m2x"])
            tt(nb[:, 0:1], m2x[:, 0:1], m2x[:, 1:2], ALU.min, ["m2x"], ["nb"])
            for h in range(4):
                bk = (pa, pb_)[h // 2]
                P.op("act", lambda e, h=h, bk=bk: e.activation(
                    out=Pf[:, h, :], in_=bank(bk)[:, 256 * (h % 2):256 * (h % 2) + 256], func=AF.Exp,
                    bias=nb[:, 0:1], scale=1.0, accum_out=ssum[:, h:h + 1]), ["b%d" % bk, "nb"], ["Pf%d" % (h // 2), "ssum%d" % h])
            recip(rsm[:, :], ssum[:, :], ["ssum0", "ssum1", "ssum2", "ssum3"], ["rsm"])
            tt(Pb[:, :, :], Pf[:, :, :], rsm[:, :].unsqueeze(2).to_broadcast([128, 4, 256]), ALU.mult, ["Pf0", "Pf1", "rsm"], ["Pb"])

        qcnt = [0]

        def q_group(t, hd):
            c0, N = TILES[t]
            a = qcnt[0] % 2
            qcnt[0] += 1
            dstq = qT[t % 2][:, hd, 0:N] if t < 4 else qTs[:, hd, 0:N]
            qk = ("qT%d" % (t % 2)) if t < 4 else "qTs"
            for kc in range(8):
                mm(bank(a)[:, 0:N], wq[:, kc, 128 * hd:128 * hd + 128], xnt[:, kc, 0:N], kc == 0, kc == 7, ["wq", "xnt"], ["b%d" % a])
            act(dstq, bank(a)[:, 0:N], AF.Copy, ["b%d" % a], [qk], scale=SCALE)

        def o_group(t, oc):
            c0, N = TILES[t]
            a = qcnt[0] % 2
            qcnt[0] += 1
            src_ = oT if t < 4 else oTs
            sk = "oT" if t < 4 else "oTs"
            for kc in range(8):
                mm(bank(a)[:, 0:N], wo[:, kc, 128 * oc:128 * oc + 128], src_[:, kc, 0:N], kc == 0, kc == 7, ["wo", sk], ["b%d" % a])
            tt(xT[:, oc, c0:c0 + N], bank(a)[:, 0:N], xT[:, oc, c0:c0 + N], ALU.add, ["b%d" % a, "x%d" % t], ["x%d" % t])

        def n_stage(t):
            norm_tile(t, 2, lambda kc, N: xnt[:, kc, 0:N], "xnt", sq, rstd, 6)

        def s_stage(t, tc, st_):
            for h in range(4):
                pb_ = SETS[st_][h // 2]
                for dc in range(2):
                    mm(bank(pb_)[:, 256 * (h % 2):256 * (h % 2) + 256], qT[t % 2][:, 2 * h + dc, 128 * tc:128 * tc + 128], KT[:, 2 * h + dc, :],
                       dc == 0, dc == 1, ["qT%d" % (t % 2), "KT"], ["b%d" % pb_])

        def t_stage(tc):
            for mc in range(2):
                for h in range(4):
                    tr(bankb(4)[:, 128 * (4 * mc + h):128 * (4 * mc + h) + 128], Pb[:, h, 128 * mc:128 * mc + 128], identb[:, :],
                       ["Pb", "identb"], ["b4"])
            evac(PT[:, :, :, 128 * tc:128 * tc + 128], bankb(4).rearrange("p (m h t) -> p m h t", m=2, h=4), ["b4"], ["PT"])

        def pv_stage():
            for hd in range(8):
                a = qcnt[0] % 2
                qcnt[0] += 1
                for mc in range(2):
                    mm(bank(a)[:, :], Vb[:, mc, 128 * hd:128 * hd + 128], PT[:, mc, hd // 2, :], mc == 0, mc == 1, ["Vb", "PT"], ["b%d" % a])
                evac(oT[:, hd, :], bank(a)[:, :], ["b%d" % a], ["oT"])

        def load_k(s):
            sl = s % 2
            ld(Ks[sl][:, :, :], ck[s].rearrange("(mc p) f -> p mc f", p=128), ["Ks%d" % sl], eng="pool")

        def load_v(s):
            sv = s % NV
            ld(Vs[sv][:, :, :], cv[s].rearrange("(mc p) f -> p mc f", p=128), ["Vs%d" % sv], eng="pool")

        def sg_A(G, si):
            s = 4 * G + si
            sl = s % 2
            for mc in range(2):
                bk = 6 if mc == 0 else 4
                for hd in range(8):
                    tr(bankb(bk)[:, 128 * hd:128 * hd + 128], Ks[sl][:, mc, 128 * hd:128 * hd + 128], identb[:, :],
                       ["Ks%d" % sl, "identb"], ["b%d" % bk])
                cp(KTs[sl][:, :, 128 * mc:128 * mc + 128], bankb(bk).rearrange("p (h m) -> p h m", h=8),
                   ["b%d" % bk], ["KTs%d" % sl], eng=("act" if mc == 0 else "dve"))
            for h in range(4):
                pb_ = SETS[1][h // 2]
                for dc in range(2):
                    mm(bank(pb_)[32 * si:32 * si + 8, 256 * (h % 2):256 * (h % 2) + 256], qTs[:, 2 * h + dc, 8 * s:8 * s + 8],
                       KTs[sl][:, 2 * h + dc, :], dc == 0, dc == 1, ["qTs", "KTs%d" % sl], ["b%d" % pb_], tile_position=(0, 32 * si))
            if s + 2 < 16:
                load_k(s + 2)

        def sg_C(G):
            for mc in range(2):
                for h in range(4):
                    tr(bankb(4)[:, 128 * (4 * mc + h):128 * (4 * mc + h) + 128], Pb[:, h, 128 * mc:128 * mc + 128], identb[:, :],
                       ["Pb", "identb"], ["b4"])
            evac(PTs[:, :, :, :], bankb(4).rearrange("p (m h t) -> p m h t", m=2, h=4), ["b4"], ["PTs"])
            for si in range(4):
                s = 4 * G + si
                sv = s % NV
                for hd in range(8):
                    for mc in range(2):
                        mm(bank(hd // 4)[:, 32 * (hd % 4) + 8 * si:32 * (hd % 4) + 8 * si + 8], Vs[sv][:, mc, 128 * hd:128 * hd + 128],
                           PTs[:, mc, hd // 2, 32 * si:32 * si + 8], mc == 0, mc == 1, ["Vs%d" % sv, "PTs"], ["b%d" % (hd // 4)])
                if s + NV < 16:
                    load_v(s + NV)
            for hh in range(2):
                evac(oTs[:, 4 * hh:4 * hh + 4, 32 * G:32 * G + 32], bank(hh)[:, 0:128].rearrange("p (h t) -> p h t", h=4),
                     ["b%d" % hh], ["oTs"])

        n_stage(0)
        for hd in range(8):
            q_group(0, hd)
        n_stage(4)
        for hd in range(8):
            q_group(4, hd)
        n_stage(1)
        chk(4.1)
        for mc in range(2):
            ld(mst[:, :], memp[128 * mc:128 * mc + 128, :], ["mst"])
            P.op("act", lambda e: e.activation(out=junk[:, :], in_=mst[:, :], func=AF.Square, accum_out=ssum[:, 0:1]),
                 ["mst"], ["junk", "ssum"])
            act(rsm[:, 0:1], ssum[:, 0:1], AF.Sqrt, ["ssum"], ["rsm"], bias=EPS, scale=1.0 / 1024.0)
            recip(rsm[:, 0:1], rsm[:, 0:1], ["rsm"], ["rsm"])
            ts1(mrb[:, :], mst[:, :], rsm[:, 0:1], ALU.mult, ["mst", "rsm"], ["mrb"])
            for kc in range(8):
                tr(bankb(7)[:, 128 * kc:128 * kc + 128], mrb[:, 128 * kc:128 * kc + 128], identb[:, :], ["mrb", "identb"], ["b7"])
            for kc in range(8):
                ts1(mnT[:, kc, 128 * mc:128 * mc + 128], bankb(7)[:, 128 * kc:128 * kc + 128], gcol(5, kc), ALU.mult,
                    ["b7", "PVT"], ["mnT"])
        chk(4.2)
        cntk = [0]
        for (wsrc, wkey, dst, isv) in ((wk, "wk", nk_p, False), (wv, "wv", nv_p, True)):
            for mc in range(2):
                s = cntk[0] % 2
                cntk[0] += 1
                for hh in range(2):
                    a = hh
                    for kc in range(8):
                        mm(bank(a)[:, :], mnT[:, kc, 128 * mc:128 * mc + 128], wsrc[:, kc, 512 * hh:512 * hh + 512],
                           kc == 0, kc == 7, ["mnT", wkey], ["b%d" % a])
                    cp(kvo[s][:, 512 * hh:512 * hh + 512], bank(a)[:, :], ["b%d" % a], ["kvo%d" % s], eng="act")
                    if isv:
                        cp(Vb[:, mc, 512 * hh:512 * hh + 512], kvo[s][:, 512 * hh:512 * hh + 512], ["kvo%d" % s], ["Vb"])
                stor(dst[128 * mc:128 * mc + 128, :], kvo[s][:, :], ["kvo%d" % s])
        chk(4.3)
        for hd in range(8):
            a = 2 + hd % 2
            for kc in range(8):
                mm(bank(a)[:, 0:256], wk[:, kc, 128 * hd:128 * hd + 128], mnT[:, kc, :], kc == 0, kc == 7, ["wk", "mnT"], ["b%d" % a])
            evac(KT[:, hd, :], bank(a)[:, 0:256], ["b%d" % a], ["KT"])
        P.barrier()
        chk(4.5)
        load_k(0); load_k(1)
        for s_ in range(NV):
            load_v(s_)
        for t in range(4):
            for tc in range(4):
                s_stage(t, tc, 0)
                softmax_rows(0)
                if t + 1 < 4:
                    q_group(t + 1, 2 * tc)
                    q_group(t + 1, 2 * tc + 1)
                if t >= 1:
                    o_group(t - 1, 2 * tc)
                    o_group(t - 1, 2 * tc + 1)
                sg_A(t, tc)
                t_stage(tc)
            softmax_rows(1)
            pv_stage()
            if t + 2 < 4:
                n_stage(t + 2)
            sg_C(t)
        for oc in range(8):
            o_group(3, oc)
        for oc in range(8):
            o_group(4, oc)
        P.barrier()

        if "x3" in dbg:
            stor(dbg_out["x3"], xT[:, :, :], ["x0", "x1", "x2", "x3", "x4"], key="dbgx3")
            P.barrier()
        chk(5)

        fin_bufs = {}

        def fin_alloc(bp_, sq_, rstd_):
            fin_bufs["yT"] = bp_.alloc([8, 512], F32)
            fin_bufs["oy"] = [bp_.alloc([1024], F32) for _ in range(2)]
            fin_bufs["sq"] = sq_
            fin_bufs["rstd"] = rstd_

        oc_ = [0]

        def fin_A(t):
            c0, N = TILES[t]
            act(fin_bufs["sq"][:, :, 0:N], xT[:, :, c0:c0 + N], AF.Square, ["x%d" % t], ["sq"])

        def fin_B(t):
            c0, N = TILES[t]
            sq_, rstd_, ys = fin_bufs["sq"], fin_bufs["rstd"], fin_bufs["yT"]
            for kc in range(8):
                mm(bank(6)[:, 0:N], onesb[:, :], sq_[:, kc, 0:N], kc == 0, kc == 7, ["onesb", "sq"], ["b6"])
            act(rstd_[:, 0:N], bank(6)[:, 0:N], AF.Sqrt, ["b6"], ["rstd"], bias=EPS, scale=1.0)
            recip(rstd_[:, 0:N], rstd_[:, 0:N], ["rstd"], ["rstd"])
            for kc in range(8):
                stt(ys[:, kc, 0:N], xT[:, kc, c0:c0 + N], gcol(4, kc), rstd_[:, 0:N], ALU.mult, ALU.mult,
                    ["x%d" % t, "PVT", "rstd"], ["yTf"])

        def fin_C(t):
            c0, N = TILES[t]
            ys = fin_bufs["yT"]
            for ch in range(N // 128):
                s_ = oc_[0] % 2
                oc_[0] += 1
                for h in range(2):
                    for k4 in range(4):
                        kc = 4 * h + k4
                        tr(bank(7)[:, 128 * k4:128 * k4 + 128], ys[:, kc, 128 * ch:128 * ch + 128], ident[:, :],
                           ["yTf", "ident"], ["b7"])
                    evac(fin_bufs["oy"][s_][:, 512 * h:512 * h + 512], bank(7)[:, :], ["b7"], ["oy%d" % s_])
                dst = y_p[c0 + 128 * ch:c0 + 128 * ch + 128, :] if t < 4 else y_s
                stor(dst, fin_bufs["oy"][s_][:, :], ["oy%d" % s_])

        lagq = []

        def final_tile(t):
            if t is None:
                fin_C(lagq[-2]); fin_B(lagq[-1]); fin_C(lagq[-1])
                return
            lagq.append(t)
            i = len(lagq) - 1
            if i >= 2:
                fin_C(lagq[i - 2])
            if i >= 1:
                fin_B(lagq[i - 1])
            fin_A(t)

        ffn(3, w_gate2, w_up2, w_down2, "f2", tail_cb=final_tile, extra=fin_alloc)
        P.emit(st)
    return nc, P


_CACHE = {}


def _prep_inputs(inp):
    f = lambda a: np.ascontiguousarray(np.asarray(a, dtype=np.float32))
    vecs = np.zeros((64, 1024), np.float32)
    vecs[0] = inp["g_ffn1"][0]; vecs[1] = inp["g_mix"][0]; vecs[2] = inp["g_xattn"][0]; vecs[3] = inp["g_ffn2"][0]
    vecs[4] = inp["g_final"]; vecs[5] = inp["g_mem"][0]
    vecs[6, 0:512] = inp["ssm_d"][0]; vecs[6, 512:] = inp["conv_b"][0]
    vecs[7, 0:512] = inp["conv_ln_g"][0]; vecs[7, 512:] = inp["conv_ln_b"][0]
    vecs[8:39, 0:512] = inp["conv_w"][0]
    shared = dict(
        vecs=vecs, w_mem_k=f(inp["w_mem_k"][0]), w_mem_v=f(inp["w_mem_v"][0]),
        w_gate1=f(inp["w_ffn1_gate"][0]), w_up1=f(inp["w_ffn1_up"][0]), w_down1=f(inp["w_ffn1_down"][0]),
        w_in=f(inp["w_in"][0]), a_re=f(inp["ssm_a_re"][0]), a_im=f(inp["ssm_a_im"][0]), log_dt=f(inp["ssm_log_dt"]),
        b_re=f(inp["ssm_b_re"][0]), b_im=f(inp["ssm_b_im"][0]), c_re=f(inp["ssm_c_re"][0]), c_im=f(inp["ssm_c_im"][0]),
        w_glu=f(inp["w_ssm_glu"][0]), w_out=f(inp["w_out"][0]), w_q=f(inp["w_mem_q"][0]), w_o=f(inp["w_mem_o"][0]),
        w_gate2=f(inp["w_ffn2_gate"][0]), w_up2=f(inp["w_ffn2_up"][0]), w_down2=f(inp["w_ffn2_down"][0]),
        ident=np.eye(128, dtype=np.float32))
    maps = []
    for c in range(8):
        sl = slice(16 * c, 16 * c + 16)
        m = dict(shared)
        m["xp"] = f(inp["x_prompt"][c])
        m["xs"] = f(inp["x_sample"][sl]).reshape(128, 1024)
        m["st_re"] = f(inp["state_ssm_re"][0, sl]).reshape(16, 2048)
        m["st_im"] = f(inp["state_ssm_im"][0, sl]).reshape(16, 2048)
        m["cconv"] = f(inp["cache_conv"][0, sl])
        m["ck"] = f(inp["cache_mem_k"][0, sl]).reshape(16, 256, 1024)
        m["cv"] = f(inp["cache_mem_v"][0, sl]).reshape(16, 256, 1024)
        m["memp"] = f(inp["mem_prompt"][c])
        maps.append(m)
    return maps


def kernel(**inputs):
    if "nc" not in _CACHE:
        _CACHE["nc"] = build_nc()[0]
    nc = _CACHE["nc"]
    maps = _prep_inputs(inputs)
    res = run_bass_kernel_spmd(nc, maps, core_ids=list(range(8)))
    R = res.results
    y_prompt = np.stack([R[c]["y_p"] for c in range(8)])
    y_sample = np.concatenate([R[c]["y_s"].reshape(16, 8, 1024) for c in range(8)])
    nre_p = np.stack([R[c]["nre_p"].reshape(32, 64) for c in range(8)])[None]
    nim_p = np.stack([R[c]["nim_p"].reshape(32, 64) for c in range(8)])[None]
    nconv_p = np.stack([R[c]["nconv_p"] for c in range(8)])[None]
    nk_p = np.stack([R[c]["nk_p"].reshape(256, 4, 256) for c in range(8)])[None]
    nv_p = np.stack([R[c]["nv_p"].reshape(256, 4, 256) for c in range(8)])[None]
    nre_s = np.concatenate([R[c]["nre_s"].reshape(16, 32, 64) for c in range(8)])[None]
    nim_s = np.concatenate([R[c]["nim_s"].reshape(16, 32, 64) for c in range(8)])[None]
    nconv_s = np.concatenate([R[c]["nconv_s"] for c in range(8)])[None]
    return (y_prompt.astype(np.float32), y_sample.astype(np.float32), nre_p, nim_p, nconv_p, nk_p, nv_p,
            nre_s, nim_s, nconv_s)
```
